# Optimizing a Trainium2 kernel written in Bass

```python
import jax, jax.numpy as jnp
from jax import lax
import numpy as np

D_MODEL = 1024
BATCH = 4
SEQ = 4096
DEPTH = 2

N_META = 16
BLK = 128
PAD = BLK - N_META
ROPE_THETA = 10000.0
EPS = 1e-6
NEG = -1e30

FOX_HEADS = 8
FOX_DH = 64

MLA_HEADS = 8
MLA_NOPE = 64
MLA_ROPE = 32
MLA_V = 64
MLA_QLORA = 384
MLA_KVLORA = 256

SWA_HEADS = 8
SWA_KV_HEADS = 2
SWA_DH = 64
WINDOW = 128

BRANCH_W = 512
N_BRANCH = 3

SPLIT_SIZES = (
    FOX_HEADS * FOX_DH, FOX_HEADS * FOX_DH, FOX_HEADS * FOX_DH, FOX_HEADS, BRANCH_W,
    MLA_QLORA, MLA_KVLORA, MLA_ROPE, BRANCH_W,
    SWA_HEADS * SWA_DH, SWA_KV_HEADS * SWA_DH, SWA_KV_HEADS * SWA_DH, BRANCH_W,
    N_BRANCH * D_MODEL,
)
N_IN = sum(SPLIT_SIZES)

kernel_name = "hybrid_fox_mla_swa_gated_branches"


def rmsnorm(x, g):
    xf = x.astype(jnp.float32)
    y = xf * lax.rsqrt(jnp.mean(xf * xf, axis=-1, keepdims=True) + EPS)
    return (y * g.astype(jnp.float32)).astype(x.dtype)


def rope(x, pos):
    half = x.shape[-1] // 2
    inv = ROPE_THETA ** (-jnp.arange(half, dtype=jnp.float32) / half)
    ang = pos.astype(jnp.float32)[:, None] * inv[None, :]
    cos = jnp.cos(ang)[None, :, None, :]
    sin = jnp.sin(ang)[None, :, None, :]
    xf = x.astype(jnp.float32)
    x1, x2 = xf[..., :half], xf[..., half:]
    return jnp.concatenate([x1 * cos - x2 * sin, x2 * cos + x1 * sin], axis=-1).astype(x.dtype)


def causal_block_attention(q, k, v, scale, log_cum=None):
    B, L, H, _ = q.shape
    nb = L // BLK
    kpos = jnp.arange(L)
    key_ok = kpos >= PAD
    qb = jnp.moveaxis(q.reshape(B, nb, BLK, H, q.shape[-1]), 1, 0)
    if log_cum is None:
        xs = (jnp.arange(nb), qb)
        ck = None
    else:
        cb = jnp.moveaxis(log_cum.reshape(B, nb, BLK, H), 1, 0)
        xs = (jnp.arange(nb), qb, cb)
        ck = jnp.swapaxes(log_cum, 1, 2)

    def block(args):
        i, qi = args[0], args[1]
        s = jnp.einsum('bqhd,bkhd->bhqk', qi, k).astype(jnp.float32) * scale
        if log_cum is not None:
            ci = jnp.swapaxes(args[2], 1, 2)
            s = s + (ci[..., :, None] - ck[..., None, :])
        qpos = i * BLK + jnp.arange(BLK)
        mask = (kpos[None, :] <= qpos[:, None]) & key_ok[None, :]
        s = jnp.where(mask, s, NEG)
        p = jax.nn.softmax(s, axis=-1)
        return jnp.einsum('bhqk,bkhd->bqhd', p.astype(v.dtype), v)

    o = lax.map(block, xs)
    return jnp.moveaxis(o, 0, 1).reshape(B, L, H, v.shape[-1])


def sliding_window_sink_attention(q, k, v, sinks):
    B, L, H, D = q.shape
    Hkv = k.shape[2]
    G = H // Hkv
    nb = L // BLK
    qb = q.reshape(B, nb, BLK, Hkv, G, D)

    def with_prev(t):
        prev = jnp.concatenate([jnp.zeros_like(t[:, :1]), t[:, :-1]], axis=1)
        return jnp.concatenate([prev, t], axis=2)

    kx = with_prev(k.reshape(B, nb, BLK, Hkv, D))
    vx = with_prev(v.reshape(B, nb, BLK, Hkv, D))
    s = jnp.einsum('bnqhgd,bnkhd->bnhgqk', qb, kx).astype(jnp.float32) * (D ** -0.5)
    blocks = jnp.arange(nb)[:, None]
    qpos = blocks * BLK + jnp.arange(BLK)[None, :]
    kpos = (blocks - 1) * BLK + jnp.arange(2 * BLK)[None, :]
    rel = qpos[:, :, None] - kpos[:, None, :]
    mask = (rel >= 0) & (rel < WINDOW) & (kpos >= PAD)[:, None, :]
    s = jnp.where(mask[None, :, None, None], s, NEG)
    sink = jnp.broadcast_to(sinks.astype(jnp.float32).reshape(1, 1, Hkv, G, 1, 1), s.shape[:-1] + (1,))
    p = jax.nn.softmax(jnp.concatenate([s, sink], axis=-1), axis=-1)[..., :-1]
    o = jnp.einsum('bnhgqk,bnkhd->bnqhgd', p.astype(v.dtype), vx)
    return o.reshape(B, L, H, D)


def hybrid_layer(x, pos, norm_g, w_in, b_f, g_cq, g_ckv, w_uq, w_ukv, sinks, w_branch, w_out):
    B, L, _ = x.shape
    h = rmsnorm(x, norm_g)
    proj = h @ w_in
    (a_q, a_k, a_v, a_f, a_z,
     b_cq, b_ckv, b_kr, b_z,
     c_q, c_k, c_v, c_z, gates) = jnp.split(proj, np.cumsum(SPLIT_SIZES)[:-1], axis=-1)

    log_f = jax.nn.log_sigmoid((a_f + b_f).astype(jnp.float32))
    log_cum = jnp.cumsum(log_f, axis=1)
    y_a = causal_block_attention(a_q.reshape(B, L, FOX_HEADS, FOX_DH),
                                 a_k.reshape(B, L, FOX_HEADS, FOX_DH),
                                 a_v.reshape(B, L, FOX_HEADS, FOX_DH),
                                 FOX_DH ** -0.5, log_cum)

    cq = rmsnorm(b_cq, g_cq)
    ckv = rmsnorm(b_ckv, g_ckv)
    qB = (cq @ w_uq).reshape(B, L, MLA_HEADS, MLA_NOPE + MLA_ROPE)
    q_b = jnp.concatenate([qB[..., :MLA_NOPE], rope(qB[..., MLA_NOPE:], pos)], axis=-1)
    kvB = (ckv @ w_ukv).reshape(B, L, MLA_HEADS, MLA_NOPE + MLA_V)
    k_rope = rope(b_kr.reshape(B, L, 1, MLA_ROPE), pos)
    k_b = jnp.concatenate([kvB[..., :MLA_NOPE],
                           jnp.broadcast_to(k_rope, (B, L, MLA_HEADS, MLA_ROPE))], axis=-1)
    v_b = kvB[..., MLA_NOPE:]
    y_b = causal_block_attention(q_b, k_b, v_b, (MLA_NOPE + MLA_ROPE) ** -0.5)

    qc = rope(c_q.reshape(B, L, SWA_HEADS, SWA_DH), pos)
    kc = rope(c_k.reshape(B, L, SWA_KV_HEADS, SWA_DH), pos)
    vc = c_v.reshape(B, L, SWA_KV_HEADS, SWA_DH)
    y_c = sliding_window_sink_attention(qc, kc, vc, sinks)

    branches = jnp.stack([y_a.reshape(B, L, BRANCH_W) * jax.nn.silu(a_z),
                          y_b.reshape(B, L, BRANCH_W) * jax.nn.silu(b_z),
                          y_c.reshape(B, L, BRANCH_W) * jax.nn.silu(c_z)], axis=2)
    proj_br = jnp.einsum('blnw,nwd->blnd', branches, w_branch)
    g = jax.nn.sigmoid(gates.reshape(B, L, N_BRANCH, D_MODEL))
    merged = jnp.sum(g * proj_br, axis=2)
    return x + merged @ w_out


def setup_inputs(seed: int = 0) -> dict:
    key = jax.random.key(seed)
    ks = jax.random.split(key, 14)
    f32 = jnp.float32
    nrm = lambda k, shape, scale: jax.random.normal(k, shape, f32) * scale
    return {
        "x": nrm(ks[0], (BATCH, SEQ, D_MODEL), 1.0),
        "meta_tokens": nrm(ks[1], (N_META, D_MODEL), 1.0),
        "norm_g": 1.0 + nrm(ks[2], (DEPTH, D_MODEL), 0.02),
        "w_in": nrm(ks[3], (DEPTH, D_MODEL, N_IN), D_MODEL ** -0.5),
        "b_f": jax.random.uniform(ks[4], (DEPTH, FOX_HEADS), f32, 1.0, 4.0),
        "g_cq": 1.0 + nrm(ks[5], (DEPTH, MLA_QLORA), 0.02),
        "g_ckv": 1.0 + nrm(ks[6], (DEPTH, MLA_KVLORA), 0.02),
        "w_uq": nrm(ks[7], (DEPTH, MLA_QLORA, MLA_HEADS * (MLA_NOPE + MLA_ROPE)), MLA_QLORA ** -0.5),
        "w_ukv": nrm(ks[8], (DEPTH, MLA_KVLORA, MLA_HEADS * (MLA_NOPE + MLA_V)), MLA_KVLORA ** -0.5),
        "sinks": nrm(ks[9], (DEPTH, SWA_HEADS), 0.5),
        "w_branch": nrm(ks[10], (DEPTH, N_BRANCH, BRANCH_W, D_MODEL), BRANCH_W ** -0.5),
        "w_out": nrm(ks[11], (DEPTH, D_MODEL, D_MODEL), D_MODEL ** -0.5),
        "final_g": 1.0 + nrm(ks[12], (D_MODEL,), 0.02),
    }


def reference(x, meta_tokens, norm_g, w_in, b_f, g_cq, g_ckv, w_uq, w_ukv, sinks, w_branch, w_out, final_g):
    B = x.shape[0]
    pad = jnp.zeros((B, PAD, D_MODEL), x.dtype)
    meta = jnp.broadcast_to(meta_tokens.astype(x.dtype)[None], (B, N_META, D_MODEL))
    h = jnp.concatenate([pad, meta, x], axis=1)
    pos = jnp.arange(h.shape[1]) - PAD
    for l in range(DEPTH):
        h = hybrid_layer(h, pos, norm_g[l], w_in[l], b_f[l], g_cq[l], g_ckv[l],
                         w_uq[l], w_ukv[l], sinks[l], w_branch[l], w_out[l])
    h = rmsnorm(h, final_g)
    return h[:, BLK:]
```

```python
import numpy as np
import ml_dtypes
from contextlib import ExitStack
import concourse.bass as bass
import concourse.mybir as mybir
from concourse.bass_utils import run_bass_kernel_spmd

F32 = mybir.dt.float32
BF16 = mybir.dt.bfloat16
AF = mybir.ActivationFunctionType
ALU = mybir.AluOpType

D = 1024
NB = 34
NS = 17
TK = NB * 128
TQ = NS * 128
PADN = 112
NIN = 7592
EPS = 1e-6
MASKV = -30000.0
C_AQ, C_AK, C_AV, C_AF, C_AZ = 0, 512, 1024, 1536, 1544
C_BCQ, C_BCKV, C_BKR, C_BZ = 2056, 2440, 2696, 2728
C_CQ, C_CK, C_CV, C_CZ, C_G = 3240, 3752, 3880, 4008, 4520
V_NG, V_GCQ, V_GCKV, V_BF, V_SK, V_FG, NV = 0, 1024, 1408, 1664, 1672, 1680, 2704
M_E, M_O, M_E0, M_P, M_SP, M_SE, M_SE0, M_TRI, M_E0F, M_PREV, M_PREVP, NMASK = 0, 1, 2, 3, 4, 5, 6, 7, 8, 9, 10, 11


class Prog:
    def __init__(self, nc, stack):
        self.nc = nc
        self.stack = stack
        self.eng = {"pe": nc.tensor, "act": nc.scalar, "dve": nc.vector, "pool": nc.gpsimd, "sp": nc.sync}
        self.sems = {}
        self.cnt = {}
        for e in ("pe", "act", "dve", "pool"):
            self.sems[e] = stack.enter_context(nc.semaphore("s_" + e))
            self.cnt[e] = 0
        self.seen = {e: {} for e in self.eng}
        self.lastw = {}
        self.readers = {}
        self.n_inst = 0

    def dma_sem(self, name):
        if name not in self.sems:
            self.sems[name] = self.stack.enter_context(self.nc.semaphore("d_" + name))
            self.cnt[name] = 0
        return name

    def _deps(self, reads, writes):
        deps = {}

        def add(d):
            if d is not None and d[1] > deps.get(d[0], 0):
                deps[d[0]] = d[1]
        for k in reads:
            add(self.lastw.get(k))
        for k in writes:
            add(self.lastw.get(k))
            for r in self.readers.get(k, ()):
                add(r)
        return deps

    def _wait(self, e, deps):
        for s, v in deps.items():
            if s == "pe" and e == "pe":
                continue
            if self.seen[e].get(s, 0) < v:
                self.eng[e].wait_ge(self.sems[s], v)
                self.seen[e][s] = v

    def _record(self, reads, writes, tok):
        for k in reads:
            lst = self.readers.setdefault(k, [])
            lst[:] = [r for r in lst if r[0] != tok[0]]
            lst.append(tok)
        for k in writes:
            self.lastw[k] = tok
            self.readers[k] = []

    def op(self, e, fn, reads=(), writes=(), inc=True):
        ps = [k for k in reads if k.startswith("pf") or k.startswith("tp")]
        if ps:
            reads = [k for k in reads if k not in ps]
            writes = list(writes) + ps
        self._wait(e, self._deps(reads, writes))
        ins = fn(self.eng[e])
        self.n_inst += 1
        self._record(reads, writes, (e, self.cnt[e] + 1))
        if inc:
            ins.then_inc(self.sems[e], 1)
            self.cnt[e] += 1
        return ins

    def dma(self, q, out, in_, reads=(), writes=(), slot=None, **kw):
        s = self.dma_sem(slot)
        self._wait(q, self._deps(reads, writes))
        ins = self.eng[q].dma_start(out=out, in_=in_, **kw)
        ins.then_inc(self.sems[s], 16)
        self.cnt[s] += 16
        self._record(reads, writes, (s, self.cnt[s]))
        self.n_inst += 1
        return ins

    def barrier(self, engines=("pe", "act", "dve", "pool", "sp")):
        deps = {s: c for s, c in self.cnt.items() if c > 0}
        for e in engines:
            self._wait(e, dict(deps))


class Rot:
    def __init__(self, items):
        self.items = list(items)
        self.i = 0

    def next(self):
        it = self.items[self.i % len(self.items)]
        self.i += 1
        return it


class Ctx:
    pass


def kkey(name, *a):
    return name + "_" + "_".join(str(x) for x in a)


PFK = [f"pf{i}" for i in range(6)]
TPK = ["tp0", "tp1"]


def qslots(C, ci):
    return list(range(4 * ci, min(4 * ci + 4, C.ns)))


def nqc(C):
    return (C.ns + 3) // 4


NKC = (NB + 3) // 4


def rstd_from_ss(C, ss_ap, out_ap_, n, keys):
    P, G = C.P, C.G
    P.op("act", lambda e: e.activation(out=out_ap_, in_=ss_ap, func=AF.Ln, scale=1.0 / n, bias=G["epsb"][:, 0:1]), reads=keys + ["epsb"], writes=keys)
    P.op("act", lambda e: e.activation(out=out_ap_, in_=out_ap_, func=AF.Exp, scale=-0.5), reads=keys, writes=keys)


def norm_block(C, src_rows, i, hnT_dst, hnT_key, gcol, keep_x=False):
    P, G = C.P, C.G
    b2 = i % 2
    xt, xk = G["xt"][b2], f"xt{b2}"
    P.dma("sp", xt[:], src_rows, writes=[xk], slot=xk)
    hb, hbk = G["hnb"][b2], f"hnb{b2}"
    P.op("act", lambda e: e.activation(out=hb[:], in_=xt[:], func=AF.Square, accum_out=G["ss"][:, b2:b2 + 1]), reads=[xk], writes=[hbk, f"ss{b2}"])
    rstd_from_ss(C, G["ss"][:, b2:b2 + 1], G["rs"][:, b2:b2 + 1], float(D), [f"ss{b2}", f"rs{b2}"])
    P.op("dve", lambda e: e.scalar_tensor_tensor(out=hb[:], in0=xt[:], scalar=G["rs"][:, b2:b2 + 1], in1=G["vec"][:, gcol:gcol + D], op0=ALU.mult, op1=ALU.mult),
         reads=[xk, f"rs{b2}", "vec"], writes=[hbk])
    tpb = G["tp"][b2]
    for kc in range(8):
        P.op("pe", lambda e, kc=kc: e.transpose(tpb[:, kc * 128:(kc + 1) * 128], hb[:, kc * 128:(kc + 1) * 128], G["ident_bf"][:]),
             reads=[hbk, "ident_bf"], writes=[TPK[b2]], inc=(kc == 7))
    src = tpb[:, 0:1024].rearrange("p (k t) -> p k t", k=8)
    if i % 2 == 0:
        P.op("act", lambda e: e.activation(out=hnT_dst, in_=src, func=AF.Copy), reads=[TPK[b2]], writes=[hnT_key])
    else:
        P.op("dve", lambda e: e.tensor_copy(out=hnT_dst, in_=src), reads=[TPK[b2]], writes=[hnT_key])


def rope_tm(C, src, dst, cos, sin, nh, half, rkeys, wkeys):
    P, G = C.P, C.G
    t = G["rtmp"]
    n = nh * half
    a = t[:, 0:n].rearrange("p (h d) -> p h d", h=nh)
    b = t[:, 256:256 + n].rearrange("p (h d) -> p h d", h=nh)
    cb = cos.unsqueeze(1).to_broadcast([128, nh, half])
    sbb = sin.unsqueeze(1).to_broadcast([128, nh, half])
    x1 = src[:, :, 0:half]
    x2 = src[:, :, half:2 * half]
    P.op("dve", lambda e: e.tensor_tensor(out=a, in0=x1, in1=cb, op=ALU.mult), reads=rkeys, writes=["rta"])
    P.op("dve", lambda e: e.tensor_tensor(out=b, in0=x2, in1=sbb, op=ALU.mult), reads=rkeys, writes=["rtb"])
    P.op("dve", lambda e: e.tensor_tensor(out=dst[:, :, 0:half], in0=a, in1=b, op=ALU.subtract), reads=["rta", "rtb"], writes=wkeys)
    P.op("dve", lambda e: e.tensor_tensor(out=a, in0=x2, in1=cb, op=ALU.mult), reads=rkeys, writes=["rta"])
    P.op("dve", lambda e: e.tensor_tensor(out=b, in0=x1, in1=sbb, op=ALU.mult), reads=rkeys, writes=["rtb"])
    P.op("dve", lambda e: e.tensor_tensor(out=dst[:, :, half:2 * half], in0=a, in1=b, op=ALU.add), reads=["rta", "rtb"], writes=wkeys)


def softplus_neg(C, u_ps, dst, rkeys, dkey):
    P, G = C.P, C.G
    P.op("dve", lambda e: e.tensor_tensor(out=dst, in0=u_ps, in1=G["vec"][:, V_BF:V_BF + 8], op=ALU.add), reads=rkeys + ["vec"], writes=[dkey])
    P.op("act", lambda e: e.activation(out=dst, in_=dst, func=AF.Exp, scale=-1.0), reads=[dkey], writes=[dkey])
    P.op("act", lambda e: e.activation(out=dst, in_=dst, func=AF.Ln, bias=G["oneb"][:, 0:1]), reads=[dkey, "oneb"], writes=[dkey])


def hn_store(C, scratch, ci, nt, buf, keys, tag):
    C.P.dma("sp", scratch.rearrange("k p t -> p k t")[:, :, ci * 512:ci * 512 + nt], buf[:, :, 0:nt], reads=keys, writes=[f"{tag}{ci}"], slot="hnw" + keys[0][:4])


def hn_load(C, scratch, ci, nt, buf, keys, tag):
    C.P.dma("sp", buf[:, :, 0:nt], scratch.rearrange("k p t -> p k t")[:, :, ci * 512:ci * 512 + nt], reads=[f"{tag}{ci}"], writes=keys, slot="hnl" + keys[0][:4])


def q_hn_chunk(C, ci, sl, buf, bufk, hmine, first):
    nt = len(sl) * 128
    keys = [kkey(bufk, bi) for bi in range(len(sl))]
    if C.full:
        hn_load(C, C.hnA, ci, nt, buf, keys, "hnA")
    elif first:
        for j in sl:
            bi = j - sl[0]
            norm_block(C, hmine[j], j, buf[:, :, bi * 128:(bi + 1) * 128], kkey(bufk, bi), V_NG)
        hn_store(C, C.hnM, ci, nt, buf, keys, "hnM")
    else:
        hn_load(C, C.hnM, ci, nt, buf, keys, "hnM")
    return keys


def load_w(C, dst, key, src):
    C.P.dma("pool", dst, src, writes=[key], slot="w_" + key)


def evac(C, i, dst, src, rk, wk):
    if i % 2 == 0:
        C.P.op("act", lambda e: e.activation(out=dst, in_=src, func=AF.Copy), reads=rk, writes=wk)
    else:
        C.P.op("dve", lambda e: e.tensor_copy(out=dst, in_=src), reads=rk, writes=wk)


def zgate(C, ci, hb, hkeys, wz, wzk, br):
    P, G = C.P, C.G
    pf = G["pf"]
    yg = G["yg"]
    nt = len(qslots(C, ci)) * 128
    for p4 in range(4):
        ps = pf[5]
        for kc in range(8):
            P.op("pe", lambda e, p4=p4, kc=kc: e.matmul(ps[:, 0:nt], lhsT=wz[:, kc, p4 * 128:(p4 + 1) * 128], rhs=hb[:, kc, 0:nt], start=(kc == 0), stop=(kc == 7)),
                 reads=hkeys + [wzk], writes=[PFK[5]], inc=(kc == 7))
        sz = G["sz"]
        P.op("act", lambda e: e.activation(out=sz[:, 0:nt], in_=ps[:, 0:nt], func=AF.Silu), reads=[PFK[5]], writes=["sz"])
        P.op("dve", lambda e, p4=p4: e.tensor_tensor(out=yg[:, p4, 0:nt], in0=yg[:, p4, 0:nt], in1=sz[:, 0:nt], op=ALU.mult), reads=["sz", "yg"], writes=["yg"])
    P.dma("sp", C.ybr[br].rearrange("k p t -> p k t")[:, :, ci * 512:ci * 512 + nt], yg[:, :, 0:nt], reads=["yg"], writes=["ybr_dram%d_%d" % (br, ci)], slot="ybr")


def normalize(C, O, Ok, c_lo, c_hi, dsts, extra_den=None):
    P, G = C.P, C.G
    pf = G["pf"]
    rd = G["rden"]
    ones_f = G["cst"][:, 256:384]
    if extra_den is not None:
        P.op("dve", lambda e: e.tensor_tensor(out=rd[64:65, c_lo:c_hi], in0=O[64:65, c_lo:c_hi], in1=extra_den, op=ALU.add), reads=[Ok, "ES"], writes=["rden"])
        P.op("dve", lambda e: e.reciprocal(out=rd[64:65, c_lo:c_hi], in_=rd[64:65, c_lo:c_hi]), reads=["rden"], writes=["rden"])
    else:
        P.op("dve", lambda e: e.reciprocal(out=rd[64:65, c_lo:c_hi], in_=O[64:65, c_lo:c_hi]), reads=[Ok], writes=["rden"])
    bc = pf[5]
    P.op("pe", lambda e: e.matmul(bc[0:64, c_lo:c_hi], lhsT=ones_f[64:65, 0:64], rhs=rd[64:65, c_lo:c_hi], start=True, stop=True), reads=["rden", "cst"], writes=[PFK[5]])
    bcs = G["bcs"]
    P.op("act", lambda e: e.activation(out=bcs[0:64, c_lo:c_hi], in_=bc[0:64, c_lo:c_hi], func=AF.Copy), reads=[PFK[5]], writes=["bcs"])
    for (lo, hi, dst) in dsts:
        P.op("dve", lambda e, lo=lo, hi=hi, dst=dst: e.tensor_tensor(out=dst, in0=O[0:64, lo:hi], in1=bcs[0:64, lo:hi], op=ALU.mult), reads=[Ok, "bcs"], writes=["yg"])


def attn_chunk(C, ci, scale, KT, V, QT, XQ, xk_fn, xr, bias_fn):
    P, G = C.P, C.G
    pf = G["pf"]
    masks, ident_bf = G["masks"], G["ident_bf"]
    ones_f = G["cst"][:, 256:384]
    sl = qslots(C, ci)
    nt = len(sl) * 128
    full = C.full
    kbmax = sl[-1] if full else min(2 * sl[-1] + 1, NB - 1)
    yg = G["yg"]
    steps = [(h, kb) for h in range(8) for kb in range(kbmax + 1)]
    n = len(steps)
    st = {}
    Obank = {}
    deferred = []

    def stageA(i):
        h, kb = steps[i]
        p4, b0 = h // 2, 64 * (h % 2)
        jmin = max(sl[0], kb if full else kb // 2)
        c0 = (jmin - sl[0]) * 128
        si = C.srot.next()
        S, Sk = pf[si], PFK[si]
        ml = []
        for j in sl:
            if j < jmin:
                continue
            if full:
                if kb == j:
                    ml.append((j, M_E0F if j == 0 else M_TRI))
                elif kb == 0:
                    ml.append((j, M_P))
            elif kb == 2 * j:
                ml.append((j, M_E0 if j == 0 else M_E))
            elif kb == 2 * j + 1:
                ml.append((j, M_O))
            elif kb == 0:
                ml.append((j, M_P))
        P.op("pe", lambda e: e.matmul(S[:, c0:nt], lhsT=KT[:, p4, kb * 128:(kb + 1) * 128], rhs=QT[:, h, c0:nt], start=True, stop=False),
             reads=[kkey("KT", p4, kb // 4), "QT"], writes=[Sk], inc=False)
        P.op("pe", lambda e: e.matmul(S[:, c0:nt], lhsT=xk_fn(h, kb), rhs=XQ[:, h, c0:nt], start=False, stop=(len(ml) == 0)),
             reads=["XK", kkey("kropeT", kb // 4), "XQ"], writes=[Sk], inc=(len(ml) == 0))
        for mi, (j, mid) in enumerate(ml):
            jc = (j - sl[0]) * 128
            P.op("pe", lambda e, jc=jc, mid=mid: e.matmul(S[:, jc:jc + 128], lhsT=ident_bf[:], rhs=masks[:, mid, :], start=False, stop=(mi == len(ml) - 1)),
                 reads=["masks", "ident_bf"], writes=[Sk], inc=(mi == len(ml) - 1))
        st[i] = (S, Sk, c0)

    def stageB(i):
        h, kb = steps[i]
        S, Sk, c0 = st[i]
        pi = C.ptrot.next()
        Pt, Ptk = G["Pt"][pi], f"Pt{pi}"
        bias = bias_fn(h, kb)
        if bias is None:
            P.op("act", lambda e: e.activation(out=Pt[:, c0:nt], in_=S[:, c0:nt], func=AF.Exp, scale=scale), reads=[Sk], writes=[Ptk])
        else:
            P.op("act", lambda e: e.activation(out=Pt[:, c0:nt], in_=S[:, c0:nt], func=AF.Exp, scale=scale, bias=bias), reads=[Sk, kkey("Ck", kb)], writes=[Ptk])
        st[i] = (Pt, Ptk, c0)

    def stageC(i, it):
        h, kb = steps[i]
        Pt, Ptk, c0 = st.pop(i)
        if kb == 0:
            while deferred and deferred[0][2] <= h - 2:
                deferred.pop(0)[1]()
            Obank[h] = C.orot.next()
        oi = Obank[h]
        O, Ok = pf[oi], PFK[oi]
        P.op("pe", lambda e: e.matmul(O[0:65, c0:nt], lhsT=V[:, kb, h, 0:65], rhs=Pt[:, c0:nt], start=(kb == 0), stop=(kb == kbmax)),
             reads=[Ptk, kkey("V", kb), "Vones"], writes=[Ok], inc=True)
        if kb == kbmax:
            p4, b0 = h // 2, 64 * (h % 2)
            r2 = h % 2
            rd, rdk = G["rden2"][r2], f"rden{r2}"
            bcs, bck = G["bcs2"][r2], f"bcs{r2}"
            P.op("dve", lambda e: e.reciprocal(out=rd[64:65, 0:nt], in_=O[64:65, 0:nt]), reads=[Ok], writes=[rdk])
            bc = pf[5]

            def t2():
                P.op("pe", lambda e: e.matmul(bc[0:64, 0:nt], lhsT=ones_f[64:65, 0:64], rhs=rd[64:65, 0:nt], start=True, stop=True), reads=[rdk, "cst"], writes=[PFK[5]])
                P.op("act", lambda e: e.activation(out=bcs[0:64, 0:nt], in_=bc[0:64, 0:nt], func=AF.Copy), reads=[PFK[5]], writes=[bck])

            def t4():
                P.op("dve", lambda e: e.tensor_tensor(out=yg[b0:b0 + 64, p4, 0:nt], in0=O[0:64, 0:nt], in1=bcs[0:64, 0:nt], op=ALU.mult), reads=[Ok, bck], writes=["yg"])
            deferred.append((it + 8, t2, h))
            deferred.append((it + 10, t4, h))

    for it in range(n + 2):
        if it < n:
            stageA(it)
        if 0 <= it - 1 < n:
            stageB(it - 1)
        if 0 <= it - 2 < n:
            stageC(it - 2, it)
        while deferred and deferred[0][0] <= it:
            deferred.pop(0)[1]()
    while deferred:
        deferred.pop(0)[1]()


def phase_fox(C, li, hfull, hmine):
    P, G, nc = C.P, C.G, C.nc
    pf, tp, vec = G["pf"], G["tp"], G["vec"]
    tri_f, ones_f, ident_f = G["cst"][:, 128:256], G["cst"][:, 256:384], G["cst"][:, 0:128]
    ident_bf = G["ident_bf"]
    wv = C.w_in[li].rearrange("(kc p) n -> p kc n", p=128)
    P.barrier()
    with ExitStack() as ps:
        sb = lambda name, shape, dt: ps.enter_context(nc.sbuf_tensor(C.pfx + name, shape, dt))
        KT = sb("f_KT", [128, 4, TK], BF16)
        V = sb("f_V", [128, NB, 8, 65], BF16)
        WKV = sb("f_WKV", [128, 8, 1032], BF16)
        WQ = sb("f_WQ", [128, 8, 520], BF16)
        WZ = sb("f_WZ", [128, 8, 512], BF16)
        Ck = sb("f_Ck", [128, NB, 8], F32)
        tot = sb("f_tot", [128, NB + 1, 8], F32)
        Xf = sb("f_Xf", [128, 512], F32)
        cq = sb("f_cq", [128, 48], F32)
        cqb = sb("f_cqb", [128, 16], BF16)
        spt = sb("f_sp", [128, 8], F32)
        load_w(C, WKV[:, :, 512:1032], "WKVb", wv[:, :, C_AV:C_AV + 520])
        load_w(C, WKV[:, :, 0:512], "WKVa", wv[:, :, C_AK:C_AK + 512])
        load_w(C, WQ[:, :, 0:512], "WQa", wv[:, :, C_AQ:C_AQ + 512])
        load_w(C, WQ[:, :, 512:520], "WQf", wv[:, :, C_AF:C_AF + 8])
        load_w(C, WZ[:, :, :], "WZ", wv[:, :, C_AZ:C_AZ + 512])
        P.op("pool", lambda e: e.memset(tot[:, 0, :], 0.0), writes=["tot"])
        P.op("pool", lambda e: e.memset(V[:, :, :, 64:65], 1.0), writes=["Vones"])
        P.op("pool", lambda e: e.memset(Xf[:], 0.0), writes=["Xf"])
        P.op("pool", lambda e: e.memset(G["XQ"][:], 0.0), writes=["XQ"])
        for ci in range(NKC):
            blks = list(range(4 * ci, min(4 * ci + 4, NB)))
            hnT, hk = G["hnT"][ci % 2], f"hnT{ci % 2}"
            for b in blks:
                bi = b - 4 * ci
                norm_block(C, hfull[b], b, hnT[:, :, bi * 128:(bi + 1) * 128], kkey(hk, bi), V_NG)
                t1, t2 = pf[0], pf[1]
                for (pt, pk, c0, n) in ((t1, PFK[0], 512, 512), (t2, PFK[1], 1024, 8)):
                    for kc in range(8):
                        P.op("pe", lambda e, pt=pt, c0=c0, n=n, kc=kc: e.matmul(pt[:, 0:n], lhsT=hnT[:, kc, bi * 128:(bi + 1) * 128], rhs=WKV[:, kc, c0:c0 + n], start=(kc == 0), stop=(kc == 7)),
                             reads=[kkey(hk, bi), "WKVb"], writes=[pk], inc=(kc == 7))
                evac(C, b, V[:, b, :, 0:64], t1[:, 0:512].rearrange("p (h d) -> p h d", h=8), [PFK[0]], [kkey("V", b)])
                softplus_neg(C, t2[:, 0:8], spt[:, 0:8], [PFK[1]], "sp")
                pc = pf[3]
                P.op("pe", lambda e: e.matmul(pc[:, 0:8], lhsT=tri_f, rhs=spt[:, 0:8], start=True, stop=True), reads=["sp", "cst"], writes=[PFK[3]], inc=False)
                P.op("pe", lambda e: e.matmul(pc[:, 8:16], lhsT=ones_f, rhs=spt[:, 0:8], start=True, stop=True), reads=["sp", "cst"], writes=[PFK[3]])
                P.op("dve", lambda e, b=b: e.tensor_tensor(out=Ck[:, b, :], in0=pc[:, 0:8], in1=tot[:, b, :], op=ALU.add), reads=[PFK[3], "tot"], writes=[kkey("Ck", b)])
                P.op("dve", lambda e, b=b: e.tensor_tensor(out=tot[:, b + 1, :], in0=pc[:, 8:16], in1=tot[:, b, :], op=ALU.add), reads=[PFK[3], "tot"], writes=["tot"])
            nt = len(blks) * 128
            hn_store(C, C.hnA, ci, nt, hnT, [kkey(hk, bi) for bi in range(len(blks))], "hnA")
            for p4 in range(4):
                psm, psk = pf[4 + p4 % 2], PFK[4 + p4 % 2]
                for kc in range(8):
                    P.op("pe", lambda e, p4=p4, kc=kc: e.matmul(psm[:, 0:nt], lhsT=WKV[:, kc, p4 * 128:(p4 + 1) * 128], rhs=hnT[:, kc, 0:nt], start=(kc == 0), stop=(kc == 7)),
                         reads=[kkey(hk, bi) for bi in range(len(blks))] + ["WKVa"], writes=[psk], inc=(kc == 7))
                evac(C, p4, KT[:, p4, ci * 512:ci * 512 + nt], psm[:, 0:nt], [psk], [kkey("KT", p4, ci)])
        QT, XQ = G["QT"], G["XQ"]
        ones_bf = G["ones_bf"]
        for ci in range(nqc(C)):
            sl = qslots(C, ci)
            nt = len(sl) * 128
            buf, bufk = G["hnT"][ci % 2], f"hnT{ci % 2}"
            hkeys = q_hn_chunk(C, ci, sl, buf, bufk, hmine, True)
            for p4 in range(4):
                psm, psk = pf[4 + p4 % 2], PFK[4 + p4 % 2]
                for kc in range(8):
                    P.op("pe", lambda e, p4=p4, kc=kc: e.matmul(psm[:, 0:nt], lhsT=WQ[:, kc, p4 * 128:(p4 + 1) * 128], rhs=buf[:, kc, 0:nt], start=(kc == 0), stop=(kc == 7)),
                         reads=hkeys + ["WQa"], writes=[psk], inc=(kc == 7))
                P.op("act", lambda e, p4=p4: e.activation(out=QT[0:64, 2 * p4, 0:nt], in_=psm[0:64, 0:nt], func=AF.Copy), reads=[psk], writes=["QT"])
                P.op("dve", lambda e, p4=p4: e.tensor_copy(out=QT[64:128, 2 * p4 + 1, 0:nt], in_=psm[64:128, 0:nt]), reads=[psk], writes=["QT"])
            for j in sl:
                bi = j - sl[0]
                t2 = pf[1]
                for kc in range(8):
                    P.op("pe", lambda e, kc=kc: e.matmul(t2[:, 0:8], lhsT=buf[:, kc, bi * 128:(bi + 1) * 128], rhs=WQ[:, kc, 512:520], start=(kc == 0), stop=(kc == 7)),
                         reads=hkeys + ["WQf"], writes=[PFK[1]], inc=(kc == 7))
                softplus_neg(C, t2[:, 0:8], spt[:, 0:8], [PFK[1]], "sp")
                pc = pf[3]
                P.op("pe", lambda e: e.matmul(pc[:, 0:8], lhsT=tri_f, rhs=spt[:, 0:8], start=True, stop=True), reads=["sp", "cst"], writes=[PFK[3]])
                if C.full:
                    P.op("dve", lambda e, j=j: e.tensor_tensor(out=cq[:, 0:8], in0=pc[:, 0:8], in1=tot[:, j, :], op=ALU.add), reads=[PFK[3], "tot"], writes=["cq0"])
                else:
                    P.op("dve", lambda e, j=j: e.tensor_tensor(out=cq[:, 0:8], in0=tot[:, 2 * j + 1, :], in1=tot[:, 2 * j, :], op=ALU.subtract), reads=["tot"], writes=["cq0"])
                    P.op("dve", lambda e, j=j: e.scalar_tensor_tensor(out=cq[:, 0:8], in0=cq[:, 0:8], scalar=G["par"][:, 0:1], in1=tot[:, 2 * j, :], op0=ALU.mult, op1=ALU.add),
                         reads=["cq0", "par", "tot"], writes=["cq0"])
                    P.op("dve", lambda e: e.tensor_tensor(out=cq[:, 0:8], in0=pc[:, 0:8], in1=cq[:, 0:8], op=ALU.add), reads=[PFK[3], "cq0"], writes=["cq0"])
                P.op("dve", lambda e: e.tensor_scalar(out=cq[:, 8:16], in0=cq[:, 0:8], scalar1=-8.0, scalar2=None, op0=ALU.mult), reads=["cq0"], writes=["cq1"])
                P.op("dve", lambda e: e.tensor_copy(out=cqb[:, 0:8], in_=cq[:, 8:16]), reads=["cq1"], writes=["cqb0"])
                P.op("dve", lambda e: e.tensor_copy(out=cq[:, 16:24], in_=cqb[:, 0:8]), reads=["cqb0"], writes=["cq2"])
                P.op("dve", lambda e: e.tensor_tensor(out=cq[:, 24:32], in0=cq[:, 8:16], in1=cq[:, 16:24], op=ALU.subtract), reads=["cq1", "cq2"], writes=["cq3"])
                P.op("dve", lambda e: e.tensor_copy(out=cqb[:, 8:16], in_=cq[:, 24:32]), reads=["cq3"], writes=["cqb1"])
                P.op("dve", lambda e: e.tensor_copy(out=cq[:, 32:40], in_=cqb[:, 8:16]), reads=["cqb1"], writes=["cq4"])
                P.op("dve", lambda e: e.tensor_tensor(out=cq[:, 40:48], in0=cq[:, 24:32], in1=cq[:, 32:40], op=ALU.subtract), reads=["cq3", "cq4"], writes=["cq5"])
                Xv = Xf[:, 0:512].rearrange("p (a b c) -> p a b c", a=4, b=2)
                for r, (c0_, ck) in enumerate(((16, "cq2"), (32, "cq4"), (40, "cq5"))):
                    P.op("dve", lambda e, r=r, c0_=c0_: e.tensor_copy(out=Xv[:, :, :, r], in_=cq[:, c0_:c0_ + 8].rearrange("p (a b) -> p a b", a=4)), reads=[ck], writes=["Xf"])
                xp = pf[2]
                for p4 in range(4):
                    P.op("pe", lambda e, p4=p4: e.transpose(xp[:, p4 * 128:(p4 + 1) * 128], Xf[:, p4 * 128:(p4 + 1) * 128], ident_f), reads=["Xf", "cst"], writes=[PFK[2]], inc=(p4 == 3))
                xpv = xp[:, 0:512].rearrange("p (a t) -> p a t", a=4)
                XQv = XQ[:, :, bi * 128:(bi + 1) * 128].rearrange("p (a e) t -> p a e t", e=2)
                P.op("act", lambda e: e.activation(out=XQv[0:3, :, 0, :], in_=xpv[0:3, :, :], func=AF.Copy), reads=[PFK[2]], writes=["XQ"])
                P.op("act", lambda e: e.activation(out=XQv[0:3, :, 1, :], in_=xpv[64:67, :, :], func=AF.Copy), reads=[PFK[2]], writes=["XQ"])
            attn_chunk(C, ci, 0.125, KT, V, QT, XQ, lambda h, kb: G["ones3"][:, 0:128], 3, lambda h, kb: Ck[:, kb, h:h + 1])
            zgate(C, ci, buf, hkeys, WZ, "WZ", 0)
        P.barrier()


def phase_mla(C, li, hfull, hmine):
    P, G, nc = C.P, C.G, C.nc
    pf, tp, vec = G["pf"], G["tp"], G["vec"]
    ident_bf = G["ident_bf"]
    wv = C.w_in[li].rearrange("(kc p) n -> p kc n", p=128)
    P.barrier()
    with ExitStack() as ps:
        sb = lambda name, shape, dt: ps.enter_context(nc.sbuf_tensor(C.pfx + name, shape, dt))
        KT = sb("m_KT", [128, 4, TK], BF16)
        V = sb("m_V", [128, NB, 8, 65], BF16)
        kropeT = sb("m_kropeT", [128, TK], BF16)
        WKV = sb("m_WKV", [128, 8, 288], BF16)
        WQ = sb("m_WQ", [128, 8, 384], BF16)
        WZ = sb("m_WZ", [128, 8, 512], BF16)
        Wk = sb("m_Wk", [128, 2, 512], BF16)
        Wv = sb("m_Wv", [128, 2, 512], BF16)
        Wqn = sb("m_Wqn", [128, 3, 512], BF16)
        Wqr = sb("m_Wqr", [128, 3, 256], BF16)
        ckc = sb("m_ckc", [128, 2, 512], BF16)
        cqnT = sb("m_cqnT", [128, 3, 512], BF16)
        cb = sb("m_cb", [128, 384], BF16)
        Xb = sb("m_Xb", [128, 512], BF16)
        ropeK = sb("m_ropeK", [128, NB, 32], F32)
        P.dma("sp", ropeK[:], C.ropeK32_d, writes=["ropeK"], slot="rk")
        if C.full:
            ropeQ = ropeK
        else:
            ropeQ = sb("m_ropeQ", [128, NS, 32], F32)
            P.dma("sp", ropeQ[:], C.ropeQ32_d, writes=["ropeQ"], slot="rq")
        load_w(C, WKV[:, :, :], "WKVa", wv[:, :, C_BCKV:C_BCKV + 288])
        ukv_v = C.w_ukv[li].rearrange("(kc p) (h t d) -> p kc t h d", p=128, h=8, t=2)
        for kc in range(2):
            load_w(C, Wv[:, kc, :].rearrange("p (h d) -> p h d", h=8), "Wv", ukv_v[:, kc, 1, :, :])
        for kc in range(2):
            load_w(C, Wk[:, kc, :].rearrange("p (h d) -> p h d", h=8), "Wk", ukv_v[:, kc, 0, :, :])
        load_w(C, WQ[:, :, :], "WQa", wv[:, :, C_BCQ:C_BCQ + 384])
        load_w(C, WZ[:, :, :], "WZ", wv[:, :, C_BZ:C_BZ + 512])
        uq_v = C.w_uq[li].rearrange("(kc p) (h d) -> p kc h d", p=128, h=8)
        for kc in range(3):
            load_w(C, Wqn[:, kc, :].rearrange("p (h d) -> p h d", h=8), "Wqn", uq_v[:, kc, :, 0:64])
            load_w(C, Wqr[:, kc, :].rearrange("p (h d) -> p h d", h=8), "Wqr", uq_v[:, kc, :, 64:96])
        P.op("pool", lambda e: e.memset(V[:, :, :, 64:65], 1.0), writes=["Vones"])
        P.op("pool", lambda e: e.memset(cb[:], 0.0), writes=["cb"])
        P.op("pool", lambda e: e.memset(Xb[:], 0.0), writes=["Xb"])
        P.op("pool", lambda e: e.memset(G["XQ"][:], 0.0), writes=["XQ"])
        for ci in range(NKC):
            blks = list(range(4 * ci, min(4 * ci + 4, NB)))
            hnT, hk = G["hnT"][ci % 2], f"hnT{ci % 2}"
            hn_load(C, C.hnA, ci, len(blks) * 128, hnT, [kkey(hk, bi) for bi in range(len(blks))], "hnA")
            for b in blks:
                bi = b - 4 * ci
                t2 = pf[1]
                for kc in range(8):
                    P.op("pe", lambda e, kc=kc: e.matmul(t2[:, 0:288], lhsT=hnT[:, kc, bi * 128:(bi + 1) * 128], rhs=WKV[:, kc, 0:288], start=(kc == 0), stop=(kc == 7)),
                         reads=[kkey(hk, bi), "WKVa"], writes=[PFK[1]], inc=(kc == 7))
                P.op("act", lambda e: e.activation(out=G["sz"][:, 0:256], in_=t2[:, 0:256], func=AF.Square, accum_out=G["ss"][:, 2:3]), reads=[PFK[1]], writes=["sz", "ss2"])
                rstd_from_ss(C, G["ss"][:, 2:3], G["rs"][:, 2:3], 256.0, ["ss2", "rs2"])
                P.op("dve", lambda e: e.scalar_tensor_tensor(out=cb[:, 0:256], in0=t2[:, 0:256], scalar=G["rs"][:, 2:3], in1=vec[:, V_GCKV:V_GCKV + 256], op0=ALU.mult, op1=ALU.mult),
                     reads=[PFK[1], "rs2", "vec"], writes=["cb"])
                rope_tm(C, t2[:, 256:288].rearrange("p (h d) -> p h d", h=1), cb[:, 256:288].rearrange("p (h d) -> p h d", h=1),
                        ropeK[:, b, 0:16], ropeK[:, b, 16:32], 1, 16, [PFK[1], "ropeK"], ["cb"])
                tq, tqk = tp[(b + 1) % 2], TPK[(b + 1) % 2]
                for k3 in range(3):
                    P.op("pe", lambda e, k3=k3: e.transpose(tq[:, k3 * 128:(k3 + 1) * 128], cb[:, k3 * 128:(k3 + 1) * 128], ident_bf[:]), reads=["cb", "ident_bf"], writes=[tqk], inc=(k3 == 2))
                P.op("dve", lambda e, bi=bi: e.tensor_copy(out=ckc[:, :, bi * 128:(bi + 1) * 128], in_=tq[:, 0:256].rearrange("p (k t) -> p k t", k=2)), reads=[tqk], writes=[kkey("ckc", bi)])
                P.op("act", lambda e, b=b: e.activation(out=kropeT[:, b * 128:(b + 1) * 128], in_=tq[:, 256:384], func=AF.Copy), reads=[tqk], writes=[kkey("kropeT", b // 4)])
                pv, pvk = pf[b % 2 * 2], PFK[b % 2 * 2]
                for kc in range(2):
                    P.op("pe", lambda e, kc=kc, bi=bi: e.matmul(pv[:, 0:512], lhsT=ckc[:, kc, bi * 128:(bi + 1) * 128], rhs=Wv[:, kc, :], start=(kc == 0), stop=(kc == 1)),
                         reads=[kkey("ckc", bi), "Wv"], writes=[pvk], inc=(kc == 1))
                evac(C, b, V[:, b, :, 0:64], pv[:, 0:512].rearrange("p (h d) -> p h d", h=8), [pvk], [kkey("V", b)])
            nt = len(blks) * 128
            ckeys = [kkey("ckc", bi) for bi in range(len(blks))]
            for p4 in range(4):
                psm, psk = pf[4 + p4 % 2], PFK[4 + p4 % 2]
                for kc in range(2):
                    P.op("pe", lambda e, p4=p4, kc=kc: e.matmul(psm[:, 0:nt], lhsT=Wk[:, kc, p4 * 128:(p4 + 1) * 128], rhs=ckc[:, kc, 0:nt], start=(kc == 0), stop=(kc == 1)),
                         reads=ckeys + ["Wk"], writes=[psk], inc=(kc == 1))
                evac(C, p4, KT[:, p4, ci * 512:ci * 512 + nt], psm[:, 0:nt], [psk], [kkey("KT", p4, ci)])
        QT, XQ = G["QT"], G["XQ"]
        for ci in range(nqc(C)):
            sl = qslots(C, ci)
            nt = len(sl) * 128
            buf, bufk = G["hnT"][ci % 2], f"hnT{ci % 2}"
            hkeys = q_hn_chunk(C, ci, sl, buf, bufk, hmine, False)
            for j in sl:
                bi = j - sl[0]
                t1 = pf[0]
                for kc in range(8):
                    P.op("pe", lambda e, kc=kc: e.matmul(t1[:, 0:384], lhsT=buf[:, kc, bi * 128:(bi + 1) * 128], rhs=WQ[:, kc, 0:384], start=(kc == 0), stop=(kc == 7)),
                         reads=hkeys + ["WQa"], writes=[PFK[0]], inc=(kc == 7))
                P.op("act", lambda e: e.activation(out=G["sz"][:, 0:384], in_=t1[:, 0:384], func=AF.Square, accum_out=G["ss"][:, 2:3]), reads=[PFK[0]], writes=["sz", "ss2"])
                rstd_from_ss(C, G["ss"][:, 2:3], G["rs"][:, 2:3], 384.0, ["ss2", "rs2"])
                P.op("dve", lambda e: e.scalar_tensor_tensor(out=cb[:, 0:384], in0=t1[:, 0:384], scalar=G["rs"][:, 2:3], in1=vec[:, V_GCQ:V_GCQ + 384], op0=ALU.mult, op1=ALU.mult),
                     reads=[PFK[0], "rs2", "vec"], writes=["cb"])
                tq, tqk = tp[j % 2], TPK[j % 2]
                for k3 in range(3):
                    P.op("pe", lambda e, k3=k3: e.transpose(tq[:, k3 * 128:(k3 + 1) * 128], cb[:, k3 * 128:(k3 + 1) * 128], ident_bf[:]), reads=["cb", "ident_bf"], writes=[tqk], inc=(k3 == 2))
                P.op("dve", lambda e, bi=bi: e.tensor_copy(out=cqnT[:, :, bi * 128:(bi + 1) * 128], in_=tq[:, 0:384].rearrange("p (k t) -> p k t", k=3)), reads=[tqk], writes=[kkey("cqnT", bi)])
                t2 = pf[1]
                for kc in range(3):
                    P.op("pe", lambda e, kc=kc, bi=bi: e.matmul(t2[:, 0:256], lhsT=cqnT[:, kc, bi * 128:(bi + 1) * 128], rhs=Wqr[:, kc, 0:256], start=(kc == 0), stop=(kc == 2)),
                         reads=[kkey("cqnT", bi), "Wqr"], writes=[PFK[1]], inc=(kc == 2))
                Xbv = Xb[:, 0:512].rearrange("p (a b c) -> p a b c", a=4, b=2)[:, :, :, 0:32].rearrange("p a b c -> p (a b) c")
                rope_tm(C, t2[:, 0:256].rearrange("p (h d) -> p h d", h=8), Xbv, ropeQ[:, j, 0:16], ropeQ[:, j, 16:32], 8, 16, [PFK[1], "ropeQ", "ropeK"], ["Xb"])
                tq2, tq2k = tp[(j + 1) % 2], TPK[(j + 1) % 2]
                for p4 in range(4):
                    P.op("pe", lambda e, p4=p4: e.transpose(tq2[:, p4 * 128:(p4 + 1) * 128], Xb[:, p4 * 128:(p4 + 1) * 128], ident_bf[:]), reads=["Xb", "ident_bf"], writes=[tq2k], inc=(p4 == 3))
                tqv = tq2[:, 0:512].rearrange("p (a t) -> p a t", a=4)
                XQv = XQ[:, :, bi * 128:(bi + 1) * 128].rearrange("p (a e) t -> p a e t", e=2)
                P.op("act", lambda e: e.activation(out=XQv[0:32, :, 0, :], in_=tqv[0:32, :, :], func=AF.Copy), reads=[tq2k], writes=["XQ"])
                P.op("act", lambda e: e.activation(out=XQv[0:32, :, 1, :], in_=tqv[64:96, :, :], func=AF.Copy), reads=[tq2k], writes=["XQ"])
            ckeys = [kkey("cqnT", bi) for bi in range(len(sl))]
            for p4 in range(4):
                psm, psk = pf[4 + p4 % 2], PFK[4 + p4 % 2]
                for kc in range(3):
                    P.op("pe", lambda e, p4=p4, kc=kc: e.matmul(psm[:, 0:nt], lhsT=Wqn[:, kc, p4 * 128:(p4 + 1) * 128], rhs=cqnT[:, kc, 0:nt], start=(kc == 0), stop=(kc == 2)),
                         reads=ckeys + ["Wqn"], writes=[psk], inc=(kc == 2))
                P.op("act", lambda e, p4=p4: e.activation(out=QT[0:64, 2 * p4, 0:nt], in_=psm[0:64, 0:nt], func=AF.Copy), reads=[psk], writes=["QT"])
                P.op("dve", lambda e, p4=p4: e.tensor_copy(out=QT[64:128, 2 * p4 + 1, 0:nt], in_=psm[64:128, 0:nt]), reads=[psk], writes=["QT"])
            attn_chunk(C, ci, 96.0 ** -0.5, KT, V, QT, XQ, lambda h, kb: kropeT[:, kb * 128:(kb + 1) * 128], 32, lambda h, kb: None)
            zgate(C, ci, buf, hkeys, WZ, "WZ", 1)
        P.barrier()


def phase_swa(C, li, hfull, hmine):
    P, G, nc = C.P, C.G, C.nc
    pf, tp, vec = G["pf"], G["tp"], G["vec"]
    ident_bf, masks = G["ident_bf"], G["masks"]
    wv = C.w_in[li].rearrange("(kc p) n -> p kc n", p=128)
    P.barrier()
    with ExitStack() as ps:
        sb = lambda name, shape, dt: ps.enter_context(nc.sbuf_tensor(C.pfx + name, shape, dt))
        KTC = sb("s_KTC", [128, 2, TK], BF16)
        VC = sb("s_VC", [128, NB, 2, 65], BF16)
        WKV = sb("s_WKV", [128, 8, 256], BF16)
        WQ = sb("s_WQ", [128, 8, 512], BF16)
        WZ = sb("s_WZ", [128, 8, 512], BF16)
        cb = sb("s_cb", [128, 256], BF16)
        Xb = sb("s_Xb", [128, 512], BF16)
        es = sb("s_es", [128, 8], F32)
        ES = sb("s_ES", [128, 2, 512], F32)
        QTs_t = sb("s_QT", [128, 4096], BF16)
        QTs = QTs_t[:, 0:4096].rearrange("p (s h t) -> p s h t", s=4, h=8)
        masks4 = sb("s_m4", [128, NMASK, 512], BF16)
        ropeK = sb("s_ropeK", [128, NB, 64], F32)
        P.dma("sp", ropeK[:], C.ropeK64_d, writes=["ropeK"], slot="rk")
        if C.full:
            ropeQ = ropeK
        else:
            ropeQ = sb("s_ropeQ", [128, NS, 64], F32)
            P.dma("sp", ropeQ[:], C.ropeQ64_d, writes=["ropeQ"], slot="rq")
        load_w(C, WKV[:, :, :], "WKVa", wv[:, :, C_CK:C_CK + 256])
        load_w(C, WQ[:, :, :], "WQa", wv[:, :, C_CQ:C_CQ + 512])
        load_w(C, WZ[:, :, :], "WZ", wv[:, :, C_CZ:C_CZ + 512])
        P.op("pool", lambda e: e.memset(VC[:, :, :, 64:65], 1.0), writes=["Vones"])
        P.op("pool", lambda e: e.memset(QTs_t[:], 0.0), writes=["QTs"])
        for g4 in range(4):
            P.op("dve", lambda e, g4=g4: e.tensor_copy(out=masks4[:, :, g4 * 128:(g4 + 1) * 128], in_=masks[:, :, :]), reads=["masks"], writes=["masks4"])
        P.op("act", lambda e: e.activation(out=es[:, 0:8], in_=vec[:, V_SK:V_SK + 8], func=AF.Exp), reads=["vec"], writes=["es"])
        P.op("pool", lambda e: e.memset(ES[:], 0.0), writes=["ES"])
        for h in range(8):
            P.op("dve", lambda e, h=h: e.tensor_scalar(out=ES[:, h // 4, (h % 4) * 128:(h % 4 + 1) * 128], in0=ES[:, h // 4, (h % 4) * 128:(h % 4 + 1) * 128],
                                                       scalar1=es[:, h:h + 1], scalar2=None, op0=ALU.add), reads=["es", "ES"], writes=["ES"])
        for ci in range(NKC):
            blks = list(range(4 * ci, min(4 * ci + 4, NB)))
            hnT, hk = G["hnT"][ci % 2], f"hnT{ci % 2}"
            hn_load(C, C.hnA, ci, len(blks) * 128, hnT, [kkey(hk, bi) for bi in range(len(blks))], "hnA")
            for b in blks:
                bi = b - 4 * ci
                t3 = pf[2]
                for kc in range(8):
                    P.op("pe", lambda e, kc=kc: e.matmul(t3[:, 0:256], lhsT=hnT[:, kc, bi * 128:(bi + 1) * 128], rhs=WKV[:, kc, 0:256], start=(kc == 0), stop=(kc == 7)),
                         reads=[kkey(hk, bi), "WKVa"], writes=[PFK[2]], inc=(kc == 7))
                P.op("act", lambda e, b=b: e.activation(out=VC[:, b, :, 0:64], in_=t3[:, 128:256].rearrange("p (h d) -> p h d", h=2), func=AF.Copy), reads=[PFK[2]], writes=[kkey("VC", b)])
                kd = cb[:, 0:256].rearrange("p (h t d) -> p h t d", h=2, t=2)
                rope_tm(C, t3[:, 0:128].rearrange("p (h d) -> p h d", h=2), kd[:, :, 0, :], ropeK[:, b, 0:32], ropeK[:, b, 32:64], 2, 32, [PFK[2], "ropeK"], ["cb"])
                P.op("dve", lambda e: e.tensor_copy(out=kd[:, :, 1, :], in_=kd[:, :, 0, :]), reads=["cb"], writes=["cb"])
                tq, tqk = tp[(b + 1) % 2], TPK[(b + 1) % 2]
                for k2 in range(2):
                    P.op("pe", lambda e, k2=k2: e.transpose(tq[:, k2 * 128:(k2 + 1) * 128], cb[:, k2 * 128:(k2 + 1) * 128], ident_bf[:]), reads=["cb", "ident_bf"], writes=[tqk], inc=(k2 == 1))
                P.op("act", lambda e, b=b: e.activation(out=KTC[:, :, b * 128:(b + 1) * 128], in_=tq[:, 0:256].rearrange("p (k t) -> p k t", k=2), func=AF.Copy), reads=[tqk], writes=[kkey("KTC", b)])
        ones_f = G["cst"][:, 256:384]
        for ci in range(nqc(C)):
            sl = qslots(C, ci)
            nt = len(sl) * 128
            buf, bufk = G["hnT"][ci % 2], f"hnT{ci % 2}"
            yg = G["yg"]
            hkeys = q_hn_chunk(C, ci, sl, buf, bufk, hmine, False)
            for j in sl:
                bi = j - sl[0]
                t1 = pf[5]
                for kc in range(8):
                    P.op("pe", lambda e, kc=kc: e.matmul(t1[:, 0:512], lhsT=buf[:, kc, bi * 128:(bi + 1) * 128], rhs=WQ[:, kc, 0:512], start=(kc == 0), stop=(kc == 7)),
                         reads=hkeys + ["WQa"], writes=[PFK[5]], inc=(kc == 7))
                rope_tm(C, t1[:, 0:512].rearrange("p (h d) -> p h d", h=8), Xb[:, 0:512].rearrange("p (h d) -> p h d", h=8), ropeQ[:, j, 0:32], ropeQ[:, j, 32:64], 8, 32, [PFK[5], "ropeQ", "ropeK"], ["Xb"])
                tq2, tq2k = tp[j % 2], TPK[j % 2]
                for p4 in range(4):
                    P.op("pe", lambda e, p4=p4: e.transpose(tq2[:, p4 * 128:(p4 + 1) * 128], Xb[:, p4 * 128:(p4 + 1) * 128], ident_bf[:]), reads=["Xb", "ident_bf"], writes=[tq2k], inc=(p4 == 3))
                tqv = tq2[:, 0:512].rearrange("p (a t) -> p a t", a=4)
                Qv = QTs[:, bi, :, :].rearrange("p (a e) t -> p a e t", e=2)
                P.op("act", lambda e: e.activation(out=Qv[0:64, :, 0, :], in_=tqv[0:64, :, :], func=AF.Copy), reads=[tq2k], writes=["QTs"])
                P.op("act", lambda e: e.activation(out=Qv[64:128, :, 1, :], in_=tqv[64:128, :, :], func=AF.Copy), reads=[tq2k], writes=["QTs"])
            steps = []
            for j in sl:
                if C.full:
                    kbs = [kb for kb in (j - 1, j) if 0 <= kb < NB]
                else:
                    kbs = [kb for kb in (2 * j - 1, 2 * j, 2 * j + 1) if 0 <= kb < NB]
                for kvh in range(2):
                    for ki, kb in enumerate(kbs):
                        steps.append((j, kvh, ki, kb, len(kbs)))
            n = len(steps)
            stt = {}
            Obank = {}
            deferred = []

            def stageA(i):
                j, kvh, ki, kb, nk = steps[i]
                bi = j - sl[0]
                si = C.srot.next()
                S, Sk = pf[si], PFK[si]
                if C.full:
                    if kb == j:
                        mid = M_E0F if j == 0 else M_TRI
                    else:
                        mid = M_PREVP if kb == 0 else M_PREV
                elif kb == 2 * j - 1:
                    mid = M_SP
                elif kb == 2 * j:
                    mid = M_SE0 if j == 0 else M_SE
                else:
                    mid = M_O
                P.op("pe", lambda e: e.matmul(S[:, 0:512], lhsT=KTC[:, kvh, kb * 128:(kb + 1) * 128], rhs=QTs[:, bi, 4 * kvh:4 * kvh + 4, :].rearrange("p h t -> p (h t)"), start=True, stop=False),
                     reads=[kkey("KTC", kb), "QTs"], writes=[Sk], inc=False)
                P.op("pe", lambda e, mid=mid: e.matmul(S[:, 0:512], lhsT=ident_bf[:], rhs=masks4[:, mid, :], start=False, stop=True),
                     reads=["masks4", "ident_bf"], writes=[Sk], inc=True)
                stt[i] = (S, Sk)

            def stageB(i):
                S, Sk = stt[i]
                pi = C.ptrot.next()
                Pt, Ptk = G["Pt"][pi], f"Pt{pi}"
                P.op("act", lambda e: e.activation(out=Pt[:, 0:512], in_=S[:, 0:512], func=AF.Exp, scale=0.125), reads=[Sk], writes=[Ptk])
                stt[i] = (Pt, Ptk)

            def stageC(i, it):
                j, kvh, ki, kb, nk = steps[i]
                bi = j - sl[0]
                u = 2 * bi + kvh
                Pt, Ptk = stt.pop(i)
                if ki == 0:
                    while deferred and deferred[0][2] <= u - 2:
                        deferred.pop(0)[1]()
                    Obank[u] = C.orot.next()
                oi = Obank[u]
                O, Ok = pf[oi], PFK[oi]
                P.op("pe", lambda e: e.matmul(O[0:65, 0:512], lhsT=VC[:, kb, kvh, 0:65], rhs=Pt[:, 0:512], start=(ki == 0), stop=(ki == nk - 1)),
                     reads=[Ptk, kkey("VC", kb), "Vones"], writes=[Ok], inc=True)
                if ki == nk - 1:
                    r2 = u % 2
                    rd, rdk = G["rden2"][r2], f"rden{r2}"
                    bcs, bck = G["bcs2"][r2], f"bcs{r2}"
                    P.op("dve", lambda e: e.tensor_tensor(out=rd[64:65, 0:512], in0=O[64:65, 0:512], in1=ES[64:65, kvh, :], op=ALU.add), reads=[Ok, "ES"], writes=[rdk])
                    P.op("dve", lambda e: e.reciprocal(out=rd[64:65, 0:512], in_=rd[64:65, 0:512]), reads=[rdk], writes=[rdk])
                    bc = pf[5]

                    def t2():
                        P.op("pe", lambda e: e.matmul(bc[0:64, 0:512], lhsT=ones_f[64:65, 0:64], rhs=rd[64:65, 0:512], start=True, stop=True), reads=[rdk, "cst"], writes=[PFK[5]])
                        P.op("act", lambda e: e.activation(out=bcs[0:64, 0:512], in_=bc[0:64, 0:512], func=AF.Copy), reads=[PFK[5]], writes=[bck])

                    def t4():
                        for g in range(4):
                            h = 4 * kvh + g
                            b0 = 64 * (h % 2)
                            P.op("dve", lambda e, g=g, h=h, b0=b0: e.tensor_tensor(out=yg[b0:b0 + 64, h // 2, bi * 128:(bi + 1) * 128], in0=O[0:64, g * 128:(g + 1) * 128],
                                                                                   in1=bcs[0:64, g * 128:(g + 1) * 128], op=ALU.mult), reads=[Ok, bck], writes=["yg"])
                    deferred.append((it + 6, t2, u))
                    deferred.append((it + 8, t4, u))

            for it in range(n + 2):
                if it < n:
                    stageA(it)
                if 0 <= it - 1 < n:
                    stageB(it - 1)
                if 0 <= it - 2 < n:
                    stageC(it - 2, it)
                while deferred and deferred[0][0] <= it:
                    deferred.pop(0)[1]()
            while deferred:
                deferred.pop(0)[1]()
            zgate(C, ci, buf, hkeys, WZ, "WZ", 2)
        P.barrier()


def phase_final(C, li, hmine, out_ap, last):
    P, G, nc = C.P, C.G, C.nc
    pf, tp, vec = G["pf"], G["tp"], G["vec"]
    wv = C.w_in[li].rearrange("(kc p) n -> p kc n", p=128)
    P.barrier()
    with ExitStack() as ps:
        sb = lambda name, shape, dt: ps.enter_context(nc.sbuf_tensor(C.pfx + name, shape, dt))
        WG = sb("z_WG", [128, 8, 3072], BF16)
        WB = sb("z_WB", [128, 12, 1024], BF16)
        WO = sb("z_WO", [128, 8, 1024], BF16)
        mT = sb("z_mT", [128, 8, 512], BF16)
        ybuf = sb("z_ybuf", [128, 12, 512], BF16)
        sg = [sb(f"z_sg{i}", [128, 512], F32) for i in range(2)]
        macc = sb("z_macc", [128, 512], F32)
        tmpm = sb("z_tmpm", [128, 512], F32)
        fg = sb("z_fg", [128, D], F32)
        if last:
            P.dma("sp", fg[:], C.fg_d, writes=["fg"], slot="fg")
        wbv = C.w_branch[li].rearrange("n (kc p) d -> p n kc d", p=128)
        FK = [[f"F{q}_{n}{t}" for n in range(3) for t in "gb"] for q in range(8)]
        for q in range(8):
            for n in range(3):
                c0 = n * 1024 + q * 128
                P.dma("pool", WG[:, :, c0:c0 + 128], wv[:, :, C_G + c0:C_G + c0 + 128], writes=[f"F{q}_{n}g"], slot=f"w_F{q}")
                P.dma("pool", WB[:, n * 4:(n + 1) * 4, q * 128:(q + 1) * 128], wbv[:, n, :, q * 128:(q + 1) * 128], writes=[f"F{q}_{n}b"], slot=f"w_F{q}")
        load_w(C, WO[:, :, :], "WO", C.w_out[li].rearrange("(kc p) n -> p kc n", p=128))
        for ci in range(nqc(C)):
            sl = qslots(C, ci)
            nt = len(sl) * 128
            buf, bufk = G["hnT"][ci % 2], f"hnT{ci % 2}"
            hkeys = q_hn_chunk(C, ci, sl, buf, bufk, hmine, False)
            for n in range(3):
                P.dma("sp", ybuf[:, n * 4:(n + 1) * 4, 0:nt], C.ybr[n].rearrange("k p t -> p k t")[:, :, ci * 512:ci * 512 + nt],
                      reads=["ybr_dram%d_%d" % (n, ci)], writes=["ybuf%d" % n], slot="ybuf%d" % n)
            for dch in range(8):
                for n in range(3):
                    Gp, Gk = pf[n % 3], PFK[n % 3]
                    for kc in range(8):
                        P.op("pe", lambda e, n=n, kc=kc: e.matmul(Gp[:, 0:nt], lhsT=WG[:, kc, n * 1024 + dch * 128:n * 1024 + (dch + 1) * 128], rhs=buf[:, kc, 0:nt], start=(kc == 0), stop=(kc == 7)),
                             reads=hkeys + FK[dch], writes=[Gk], inc=(kc == 7))
                    sgn, sgk = sg[n % 2], f"sg{n % 2}"
                    P.op("act", lambda e: e.activation(out=sgn[:, 0:nt], in_=Gp[:, 0:nt], func=AF.Sigmoid), reads=[Gk], writes=[sgk])
                    Pn, Pnk = pf[3 + n % 2], PFK[3 + n % 2]
                    for kc in range(4):
                        P.op("pe", lambda e, n=n, kc=kc: e.matmul(Pn[:, 0:nt], lhsT=WB[:, n * 4 + kc, dch * 128:(dch + 1) * 128], rhs=ybuf[:, n * 4 + kc, 0:nt], start=(kc == 0), stop=(kc == 3)),
                             reads=["ybuf%d" % n] + FK[dch], writes=[Pnk], inc=(kc == 3))
                    if n == 0:
                        P.op("dve", lambda e: e.tensor_tensor(out=macc[:, 0:nt], in0=Pn[:, 0:nt], in1=sgn[:, 0:nt], op=ALU.mult), reads=[Pnk, sgk], writes=["macc"])
                    else:
                        P.op("dve", lambda e: e.tensor_tensor(out=tmpm[:, 0:nt], in0=Pn[:, 0:nt], in1=sgn[:, 0:nt], op=ALU.mult), reads=[Pnk, sgk], writes=["tmpm"])
                        if n == 1:
                            P.op("pool", lambda e: e.tensor_tensor(out=macc[:, 0:nt], in0=macc[:, 0:nt], in1=tmpm[:, 0:nt], op=ALU.add), reads=["tmpm", "macc"], writes=["macc"])
                        else:
                            P.op("pool", lambda e, dch=dch: e.tensor_tensor(out=mT[:, dch, 0:nt], in0=macc[:, 0:nt], in1=tmpm[:, 0:nt], op=ALU.add), reads=["tmpm", "macc"], writes=[kkey("mT", dch)])
            mkeys = [kkey("mT", dch) for dch in range(8)]
            for j in sl:
                bi = j - sl[0]
                xk_t, xkk = G["xt"][j % 2], f"xt{j % 2}"
                P.dma("sp", xk_t[:], hmine[j], writes=[xkk], slot=xkk)
                for hlf in range(2):
                    Ob, Obk = pf[hlf], PFK[hlf]
                    for dch in range(8):
                        P.op("pe", lambda e, dch=dch, hlf=hlf: e.matmul(Ob[:, 0:512], lhsT=mT[:, dch, bi * 128:(bi + 1) * 128], rhs=WO[:, dch, hlf * 512:(hlf + 1) * 512], start=(dch == 0), stop=(dch == 7)),
                             reads=mkeys + ["WO"], writes=[Obk], inc=(dch == 7))
                    P.op("dve", lambda e, hlf=hlf: e.tensor_tensor(out=xk_t[:, hlf * 512:(hlf + 1) * 512], in0=Ob[:, 0:512], in1=xk_t[:, hlf * 512:(hlf + 1) * 512], op=ALU.add),
                         reads=[Obk, xkk], writes=[xkk])
                if last:
                    hb, hbk = G["hnb"][j % 2], f"hnb{j % 2}"
                    P.op("act", lambda e: e.activation(out=hb[:], in_=xk_t[:], func=AF.Square, accum_out=G["ss"][:, 3:4]), reads=[xkk], writes=[hbk, "ss3"])
                    rstd_from_ss(C, G["ss"][:, 3:4], G["rs"][:, 3:4], float(D), ["ss3", "rs3"])
                    P.op("dve", lambda e: e.scalar_tensor_tensor(out=xk_t[:], in0=xk_t[:], scalar=G["rs"][:, 3:4], in1=fg[:], op0=ALU.mult, op1=ALU.mult),
                         reads=[xkk, "rs3", "fg"], writes=[xkk])
                P.dma("sp", out_ap[j], xk_t[:], reads=[xkk], writes=["out_dram"], slot=f"outw{j % 2}")
        P.barrier()


def build_program(n_layers_here, is_final, phases=("fox", "mla", "swa", "final"), ybr_kind="Internal"):
    nc = bass.Bass("TRN2", target_bir_lowering=False)
    C = Ctx()
    C.nc = nc
    C.full = False
    C.ns = NS
    C.pfx = ""
    hfull = nc.dram_tensor("hfull", [NB, 128, D], F32, kind="ExternalInput").ap()
    hmine = nc.dram_tensor("hmine", [NS, 128, D], F32, kind="ExternalInput").ap()
    C.w_in = nc.dram_tensor("w_in", [n_layers_here, D, NIN], F32, kind="ExternalInput").ap()
    C.w_uq = nc.dram_tensor("w_uq", [n_layers_here, 384, 768], F32, kind="ExternalInput").ap()
    C.w_ukv = nc.dram_tensor("w_ukv", [n_layers_here, 256, 1024], F32, kind="ExternalInput").ap()
    C.w_branch = nc.dram_tensor("w_branch", [n_layers_here, 3, 512, D], F32, kind="ExternalInput").ap()
    C.w_out = nc.dram_tensor("w_out", [n_layers_here, D, D], F32, kind="ExternalInput").ap()
    vec_d = nc.dram_tensor("vecs", [n_layers_here, 128, V_FG], F32, kind="ExternalInput").ap()
    C.fg_d = nc.dram_tensor("fg", [128, D], F32, kind="ExternalInput").ap()
    cst_d = nc.dram_tensor("cst", [128, 384], F32, kind="ExternalInput").ap()
    masks_d = nc.dram_tensor("masks", [128, NMASK, 128], BF16, kind="ExternalInput").ap()
    par_d = nc.dram_tensor("par", [128, 1], F32, kind="ExternalInput").ap()
    C.ropeK64_d = nc.dram_tensor("ropeK64", [128, NB, 64], F32, kind="ExternalInput").ap()
    C.ropeK32_d = nc.dram_tensor("ropeK32", [128, NB, 32], F32, kind="ExternalInput").ap()
    C.ropeQ64_d = nc.dram_tensor("ropeQ64", [128, NS, 64], F32, kind="ExternalInput").ap()
    C.ropeQ32_d = nc.dram_tensor("ropeQ32", [128, NS, 32], F32, kind="ExternalInput").ap()
    out_d = nc.dram_tensor("out", [NS, 128, D], F32, kind="ExternalOutput").ap()
    C.hnA = nc.dram_tensor("hnA_s", [8, 128, TK], BF16, kind="Internal").ap()
    C.hnM = nc.dram_tensor("hnM_s", [8, 128, TQ], BF16, kind="Internal").ap()
    C.ybr = nc.dram_tensor("ybr_in" if ybr_kind == "ExternalInput" else "ybr_s", [3, 4, 128, TK], BF16, kind=ybr_kind).ap()

    with ExitStack() as st:
        P = Prog(nc, st)
        C.P = P
        G = {}
        C.G = G
        sb = lambda name, shape, dt: st.enter_context(nc.sbuf_tensor(name, shape, dt))
        G["pf"] = [st.enter_context(nc.psum_tensor(f"pf{i}", [128, 512], F32)) for i in range(6)]
        G["tp"] = [st.enter_context(nc.psum_tensor(f"tp{i}", [128, 1024], BF16)) for i in range(2)]
        G["vec"] = sb("vec", [128, V_FG], F32)
        G["cst"] = sb("cstt", [128, 384], F32)
        G["masks"] = sb("maskt", [128, NMASK, 128], BF16)
        G["ident_bf"] = sb("identb", [128, 128], BF16)
        G["ones_bf"] = sb("onesb", [128, 128], BF16)
        G["par"] = sb("part", [128, 1], F32)
        G["epsb"] = sb("epsb", [128, 1], F32)
        G["oneb"] = sb("oneb", [128, 1], F32)
        G["xt"] = [sb(f"xt{i}", [128, D], F32) for i in range(2)]
        G["hnb"] = [sb(f"hnb{i}", [128, D], BF16) for i in range(2)]
        G["ss"] = sb("ss", [128, 4], F32)
        G["rs"] = sb("rs", [128, 4], F32)
        G["hnT"] = [sb(f"hnT{i}", [128, 8, 512], BF16) for i in range(2)]
        G["rtmp"] = sb("rtmp", [128, 512], F32)
        G["QT"] = sb("QT", [128, 8, 512], BF16)
        G["XQ"] = sb("XQ", [128, 8, 512], BF16)
        G["ones3"] = sb("ones3", [128, 128], BF16)
        G["Pt"] = [sb(f"Pt{i}", [128, 512], BF16) for i in range(3)]
        G["yg"] = sb("yg", [128, 4, 512], BF16)
        G["sz"] = sb("sz", [128, 512], BF16)
        G["rden"] = sb("rden", [128, 512], F32)
        G["bcs"] = sb("bcs", [64, 512], F32)
        G["rden2"] = [sb(f"rden2_{i}", [128, 512], F32) for i in range(2)]
        G["bcs2"] = [sb(f"bcs2_{i}", [64, 512], F32) for i in range(2)]
        C.srot = Rot([0, 1, 2])
        C.orot = Rot([3, 4])
        C.ptrot = Rot([0, 1, 2])

        P.dma("sp", G["cst"][:], cst_d, writes=["cst"], slot="c0")
        P.dma("sp", G["masks"][:], masks_d, writes=["masks"], slot="c1")
        P.dma("sp", G["par"][:], par_d, writes=["par"], slot="c2")
        P.op("dve", lambda e: e.tensor_copy(out=G["ident_bf"][:], in_=G["cst"][:, 0:128]), reads=["cst"], writes=["ident_bf"])
        P.op("dve", lambda e: e.tensor_copy(out=G["ones_bf"][:], in_=G["cst"][:, 256:384]), reads=["cst"], writes=["XK"])
        P.op("pool", lambda e: e.memset(G["QT"][:], 0.0), writes=["QT"])
        P.op("pool", lambda e: e.memset(G["ones3"][:], 0.0), writes=["XK"])
        P.op("pool", lambda e: e.memset(G["ones3"][0:3, :], 1.0), writes=["XK"])
        P.op("pool", lambda e: e.memset(G["epsb"][:], EPS), writes=["epsb"])
        P.op("pool", lambda e: e.memset(G["oneb"][:], 1.0), writes=["oneb"])
        for li in range(n_layers_here):
            P.barrier()
            P.dma("sp", G["vec"][:], vec_d[li], writes=["vec"], slot="vecl")
            last = is_final and (li == n_layers_here - 1)
            if "fox" in phases:
                phase_fox(C, li, hfull, hmine)
            if "mla" in phases:
                phase_mla(C, li, hfull, hmine)
            if "swa" in phases:
                phase_swa(C, li, hfull, hmine)
            if "final" in phases:
                phase_final(C, li, hmine, out_d, last)
        P.barrier()
    C.n_inst = P.n_inst
    return nc


def build_fused():
    n_layers_here, is_final, phases, ybr_kind = 2, True, ("fox", "mla", "swa", "final"), "Internal"
    nc = bass.Bass("TRN2", target_bir_lowering=False)
    C = Ctx()
    C.nc = nc
    C.full = False
    C.ns = NS
    C.pfx = ""
    hfull = nc.dram_tensor("hfull", [NB, 128, D], F32, kind="ExternalInput").ap()
    h1 = nc.dram_tensor("h1_s", [NB, 128, D], F32, kind="Internal").ap()
    hm2 = nc.dram_tensor("hm2_s", [NS, 128, D], F32, kind="Internal").ap()
    C.w_in = nc.dram_tensor("w_in", [n_layers_here, D, NIN], F32, kind="ExternalInput").ap()
    C.w_uq = nc.dram_tensor("w_uq", [n_layers_here, 384, 768], F32, kind="ExternalInput").ap()
    C.w_ukv = nc.dram_tensor("w_ukv", [n_layers_here, 256, 1024], F32, kind="ExternalInput").ap()
    C.w_branch = nc.dram_tensor("w_branch", [n_layers_here, 3, 512, D], F32, kind="ExternalInput").ap()
    C.w_out = nc.dram_tensor("w_out", [n_layers_here, D, D], F32, kind="ExternalInput").ap()
    vec_d = nc.dram_tensor("vecs", [n_layers_here, 128, V_FG], F32, kind="ExternalInput").ap()
    C.fg_d = nc.dram_tensor("fg", [128, D], F32, kind="ExternalInput").ap()
    cst_d = nc.dram_tensor("cst", [128, 384], F32, kind="ExternalInput").ap()
    masks_d = nc.dram_tensor("masks", [128, NMASK, 128], BF16, kind="ExternalInput").ap()
    par_d = nc.dram_tensor("par", [128, 1], F32, kind="ExternalInput").ap()
    C.ropeK64_d = nc.dram_tensor("ropeK64", [128, NB, 64], F32, kind="ExternalInput").ap()
    C.ropeK32_d = nc.dram_tensor("ropeK32", [128, NB, 32], F32, kind="ExternalInput").ap()
    C.ropeQ64_d = nc.dram_tensor("ropeQ64", [128, NS, 64], F32, kind="ExternalInput").ap()
    C.ropeQ32_d = nc.dram_tensor("ropeQ32", [128, NS, 32], F32, kind="ExternalInput").ap()
    out_d = nc.dram_tensor("out", [NS, 128, D], F32, kind="ExternalOutput").ap()
    C.hnA = nc.dram_tensor("hnA_s", [8, 128, TK], BF16, kind="Internal").ap()
    C.hnM = nc.dram_tensor("hnM_s", [8, 128, TQ], BF16, kind="Internal").ap()
    C.ybr = nc.dram_tensor("ybr_in" if ybr_kind == "ExternalInput" else "ybr_s", [3, 4, 128, TK], BF16, kind=ybr_kind).ap()

    with ExitStack() as st:
        P = Prog(nc, st)
        C.P = P
        G = {}
        C.G = G
        sb = lambda name, shape, dt: st.enter_context(nc.sbuf_tensor(name, shape, dt))
        G["pf"] = [st.enter_context(nc.psum_tensor(f"pf{i}", [128, 512], F32)) for i in range(6)]
        G["tp"] = [st.enter_context(nc.psum_tensor(f"tp{i}", [128, 1024], BF16)) for i in range(2)]
        G["vec"] = sb("vec", [128, V_FG], F32)
        G["cst"] = sb("cstt", [128, 384], F32)
        G["masks"] = sb("maskt", [128, NMASK, 128], BF16)
        G["ident_bf"] = sb("identb", [128, 128], BF16)
        G["ones_bf"] = sb("onesb", [128, 128], BF16)
        G["par"] = sb("part", [128, 1], F32)
        G["epsb"] = sb("epsb", [128, 1], F32)
        G["oneb"] = sb("oneb", [128, 1], F32)
        G["xt"] = [sb(f"xt{i}", [128, D], F32) for i in range(2)]
        G["hnb"] = [sb(f"hnb{i}", [128, D], BF16) for i in range(2)]
        G["ss"] = sb("ss", [128, 4], F32)
        G["rs"] = sb("rs", [128, 4], F32)
        G["hnT"] = [sb(f"hnT{i}", [128, 8, 512], BF16) for i in range(2)]
        G["rtmp"] = sb("rtmp", [128, 512], F32)
        G["QT"] = sb("QT", [128, 8, 512], BF16)
        G["XQ"] = sb("XQ", [128, 8, 512], BF16)
        G["ones3"] = sb("ones3", [128, 128], BF16)
        G["Pt"] = [sb(f"Pt{i}", [128, 512], BF16) for i in range(3)]
        G["yg"] = sb("yg", [128, 4, 512], BF16)
        G["sz"] = sb("sz", [128, 512], BF16)
        G["rden"] = sb("rden", [128, 512], F32)
        G["bcs"] = sb("bcs", [64, 512], F32)
        G["rden2"] = [sb(f"rden2_{i}", [128, 512], F32) for i in range(2)]
        G["bcs2"] = [sb(f"bcs2_{i}", [64, 512], F32) for i in range(2)]
        C.srot = Rot([0, 1, 2])
        C.orot = Rot([3, 4])
        C.ptrot = Rot([0, 1, 2])

        P.dma("sp", G["cst"][:], cst_d, writes=["cst"], slot="c0")
        P.dma("sp", G["masks"][:], masks_d, writes=["masks"], slot="c1")
        P.dma("sp", G["par"][:], par_d, writes=["par"], slot="c2")
        P.op("dve", lambda e: e.tensor_copy(out=G["ident_bf"][:], in_=G["cst"][:, 0:128]), reads=["cst"], writes=["ident_bf"])
        P.op("dve", lambda e: e.tensor_copy(out=G["ones_bf"][:], in_=G["cst"][:, 256:384]), reads=["cst"], writes=["XK"])
        P.op("pool", lambda e: e.memset(G["QT"][:], 0.0), writes=["QT"])
        P.op("pool", lambda e: e.memset(G["ones3"][:], 0.0), writes=["XK"])
        P.op("pool", lambda e: e.memset(G["ones3"][0:3, :], 1.0), writes=["XK"])
        P.op("pool", lambda e: e.memset(G["epsb"][:], EPS), writes=["epsb"])
        P.op("pool", lambda e: e.memset(G["oneb"][:], 1.0), writes=["oneb"])
        G["parn"] = sb("parn", [128, 1], F32)
        P.op("dve", lambda e: e.tensor_scalar(out=G["parn"][:], in0=G["par"][:], scalar1=-1.0, scalar2=1.0, op0=ALU.mult, op1=ALU.add), reads=["par"], writes=["parn"])
        P.barrier()
        P.dma("sp", G["vec"][:], vec_d[0], writes=["vec"], slot="vecl")
        C.full, C.ns, C.pfx = True, NB - 1, "a_"
        phase_fox(C, 0, hfull, hfull)
        phase_mla(C, 0, hfull, hfull)
        phase_swa(C, 0, hfull, hfull)
        phase_final(C, 0, hfull, h1, False)
        P.barrier()
        P.op("pool", lambda e: e.memset(G["xt"][0][:], 0.0), writes=["xt0"])
        P.dma("sp", h1[NB - 1], G["xt"][0][:], reads=["xt0"], writes=["h1z"], slot="outw")
        P.barrier()
        with ExitStack() as ts:
            tb = [[ts.enter_context(nc.sbuf_tensor(f"sel{r}{t}", [128, D], F32)) for t in "ab"] for r in range(3)]

            def sel_ld(j):
                r = j % 3
                P.dma("sp", tb[r][0][:], h1[2 * j], writes=[f"sela{r}"], slot=f"sel{r}")
                P.dma("sp", tb[r][1][:], h1[2 * j + 1], writes=[f"selb{r}"], slot=f"sel{r}")

            def sel_cs(j):
                r = j % 3
                xa, xb_ = tb[r]
                ka, kb_ = f"sela{r}", f"selb{r}"
                P.op("dve", lambda e: e.tensor_scalar(out=xa[:], in0=xa[:], scalar1=G["parn"][:, 0:1], scalar2=None, op0=ALU.mult), reads=[ka, kb_, "parn"], writes=[ka])
                P.op("dve", lambda e: e.scalar_tensor_tensor(out=xa[:], in0=xb_[:], scalar=G["par"][:, 0:1], in1=xa[:], op0=ALU.mult, op1=ALU.add), reads=[ka, kb_, "par"], writes=[ka])
                P.dma("sp", hm2[j], xa[:], reads=[ka], writes=[f"hm2_{j}"], slot="outw")

            sel_ld(0)
            sel_ld(1)
            for j in range(NS):
                if j + 2 < NS:
                    sel_ld(j + 2)
                sel_cs(j)
            P.barrier()
        P.barrier()
        P.dma("sp", G["vec"][:], vec_d[1], writes=["vec"], slot="vecl")
        C.full, C.ns, C.pfx = False, NS, "b_"
        phase_fox(C, 1, h1, hm2)
        phase_mla(C, 1, h1, hm2)
        phase_swa(C, 1, h1, hm2)
        phase_final(C, 1, hm2, out_d, True)
        P.barrier()
    C.n_inst = P.n_inst
    return nc


def _rope_tab(pos, half):
    inv = (10000.0 ** (-np.arange(half, dtype=np.float32) / half)).astype(np.float32)
    ang = pos.astype(np.float32)[:, None] * inv[None, :]
    return np.concatenate([np.cos(ang), np.sin(ang)], axis=1).astype(np.float32)


def _const_inputs():
    k = np.arange(128)[:, None]
    q = np.arange(128)[None, :]
    tri = (k <= q)
    prev = (k > q)
    padk = (k >= PADN) & (q >= 0)
    padfix = tri & ((k >= PADN) | (q < PADN))
    allv = np.ones((128, 128), bool)
    none = np.zeros((128, 128), bool)
    per_par = []
    for c in range(2):
        m = np.zeros((128, NMASK, 128), np.float32)
        valid = {
            M_E: tri if c == 0 else allv,
            M_O: none if c == 0 else tri,
            M_E0: padfix if c == 0 else padk,
            M_P: padk,
            M_SP: prev if c == 0 else none,
            M_SE: tri if c == 0 else prev,
            M_SE0: padfix if c == 0 else (prev & padk),
            M_TRI: tri, M_E0F: padfix, M_PREV: prev, M_PREVP: prev & padk,
        }
        for i, v in valid.items():
            m[:, i, :] = np.where(v, 0.0, MASKV)
        per_par.append(m.astype(ml_dtypes.bfloat16))
    cst = np.concatenate([np.eye(128, dtype=np.float32), tri.astype(np.float32), np.ones((128, 128), np.float32)], axis=1)
    pos = np.arange(TK) - PADN
    r64 = _rope_tab(pos, 32).reshape(NB, 128, 64)
    r32 = _rope_tab(pos, 16).reshape(NB, 128, 32)
    return per_par, cst, r64, r32


def _blocks_for(c):
    return [2 * j + c for j in range(NS)]


def _run_layers(hfull_list, hmine_list, inputs, layers, is_final, phases=("fox", "mla", "swa", "final"), ybr_kind="Internal", ybr_in=None):
    per_par, cst, r64, r32 = _const_inputs()
    nl = len(layers)
    nc = build_program(nl, is_final, phases=phases, ybr_kind=ybr_kind)
    vecs = np.zeros((nl, 128, V_FG), np.float32)
    for i, l in enumerate(layers):
        row = np.concatenate([inputs["norm_g"][l], inputs["g_cq"][l], inputs["g_ckv"][l], inputs["b_f"][l], inputs["sinks"][l]]).astype(np.float32)
        vecs[i] = np.broadcast_to(row[None, :], (128, V_FG))
    fg = np.ascontiguousarray(np.broadcast_to(inputs["final_g"][None, :], (128, D))).astype(np.float32)
    in_maps = []
    for core in range(8):
        c = core % 2
        blks = _blocks_for(c)
        in_maps.append({
            "hfull": hfull_list[core // 2],
            "hmine": hmine_list[core],
            "w_in": np.ascontiguousarray(inputs["w_in"][layers]),
            "w_uq": np.ascontiguousarray(inputs["w_uq"][layers]),
            "w_ukv": np.ascontiguousarray(inputs["w_ukv"][layers]),
            "w_branch": np.ascontiguousarray(inputs["w_branch"][layers]),
            "w_out": np.ascontiguousarray(inputs["w_out"][layers]),
            "vecs": vecs,
            "fg": fg,
            "cst": cst,
            "masks": per_par[c],
            "par": np.full((128, 1), float(c), np.float32),
            "ropeK64": np.ascontiguousarray(r64.transpose(1, 0, 2)),
            "ropeK32": np.ascontiguousarray(r32.transpose(1, 0, 2)),
            "ropeQ64": np.ascontiguousarray(r64[blks].transpose(1, 0, 2)),
            "ropeQ32": np.ascontiguousarray(r32[blks].transpose(1, 0, 2)),
        })
    if ybr_in is not None:
        for core in range(8):
            in_maps[core]["ybr_in"] = ybr_in[core]
    res = run_bass_kernel_spmd(nc, in_maps, core_ids=list(range(8)))
    if ybr_kind == "ExternalOutput":
        return [r["ybr_s"] for r in res.results]
    return [r["out"] for r in res.results]


def _split(h):
    B = h.shape[0]
    hb = h.reshape(B, NB, 128, D)
    hfull = [np.ascontiguousarray(hb[b]) for b in range(B)]
    hmine = []
    for core in range(2 * B):
        hmine.append(np.ascontiguousarray(hb[core // 2][_blocks_for(core % 2)]))
    return hfull, hmine


def kernel_2launch(x, meta_tokens, norm_g, w_in, b_f, g_cq, g_ckv, w_uq, w_ukv, sinks, w_branch, w_out, final_g):
    inputs = dict(norm_g=np.asarray(norm_g, np.float32), w_in=np.asarray(w_in, np.float32), b_f=np.asarray(b_f, np.float32),
                  g_cq=np.asarray(g_cq, np.float32), g_ckv=np.asarray(g_ckv, np.float32), w_uq=np.asarray(w_uq, np.float32),
                  w_ukv=np.asarray(w_ukv, np.float32), sinks=np.asarray(sinks, np.float32), w_branch=np.asarray(w_branch, np.float32),
                  w_out=np.asarray(w_out, np.float32), final_g=np.asarray(final_g, np.float32))
    x = np.asarray(x, np.float32)
    B = x.shape[0]
    h = np.zeros((B, TK, D), np.float32)
    h[:, PADN:128] = np.asarray(meta_tokens, np.float32)[None]
    h[:, 128:128 + 4096] = x
    cur = h
    for l in range(2):
        hfull, hmine = _split(cur)
        outs = _run_layers(hfull, hmine, inputs, [l], is_final=(l == 1))
        nxt = np.zeros((B, NB, 128, D), np.float32)
        for core in range(8):
            nxt[core // 2, _blocks_for(core % 2)] = outs[core]
        nxt[:, NB - 1] = 0.0
        cur = nxt.reshape(B, TK, D)
    return np.ascontiguousarray(cur[:, 128:128 + 4096])


def kernel(x, meta_tokens, norm_g, w_in, b_f, g_cq, g_ckv, w_uq, w_ukv, sinks, w_branch, w_out, final_g):
    f = lambda a: np.ascontiguousarray(np.asarray(a, np.float32))
    x = f(x)
    B = x.shape[0]
    h = np.zeros((B, TK, D), np.float32)
    h[:, PADN:128] = f(meta_tokens)[None]
    h[:, 128:128 + 4096] = x
    hb = h.reshape(B, NB, 128, D)
    per_par, cst, r64, r32 = _const_inputs()
    vecs = np.zeros((2, 128, V_FG), np.float32)
    for l in range(2):
        row = np.concatenate([f(norm_g)[l], f(g_cq)[l], f(g_ckv)[l], f(b_f)[l], f(sinks)[l]])
        vecs[l] = np.broadcast_to(row[None, :], (128, V_FG))
    fg = np.ascontiguousarray(np.broadcast_to(f(final_g)[None, :], (128, D)))
    nc = build_fused()
    in_maps = []
    for core in range(8):
        c = core % 2
        blks = _blocks_for(c)
        in_maps.append({
            "hfull": np.ascontiguousarray(hb[core // 2]),
            "w_in": f(w_in), "w_uq": f(w_uq), "w_ukv": f(w_ukv), "w_branch": f(w_branch), "w_out": f(w_out),
            "vecs": vecs, "fg": fg, "cst": cst, "masks": per_par[c],
            "par": np.full((128, 1), float(c), np.float32),
            "ropeK64": np.ascontiguousarray(r64.transpose(1, 0, 2)),
            "ropeK32": np.ascontiguousarray(r32.transpose(1, 0, 2)),
            "ropeQ64": np.ascontiguousarray(r64[blks].transpose(1, 0, 2)),
            "ropeQ32": np.ascontiguousarray(r32[blks].transpose(1, 0, 2)),
        })
    res = run_bass_kernel_spmd(nc, in_maps, core_ids=list(range(8)))
    full = np.zeros((B, NB, 128, D), np.float32)
    for core in range(8):
        full[core // 2, _blocks_for(core % 2)] = res.results[core]["out"]
    return np.ascontiguousarray(full.reshape(B, TK, D)[:, 128:128 + 4096])
```

```python
import numpy as np
import ml_dtypes
from contextlib import ExitStack
import concourse.bass as bass
import concourse.mybir as mybir
from concourse.bass_utils import run_bass_kernel_spmd

F32 = mybir.dt.float32
BF16 = mybir.dt.bfloat16
AF = mybir.ActivationFunctionType
ALU = mybir.AluOpType

D = 1024
NB = 34
NS = 17
TK = NB * 128
TQ = NS * 128
PADN = 112
NIN = 7592
EPS = 1e-6
MASKV = -30000.0
C_AQ, C_AK, C_AV, C_AF, C_AZ = 0, 512, 1024, 1536, 1544
C_BCQ, C_BCKV, C_BKR, C_BZ = 2056, 2440, 2696, 2728
C_CQ, C_CK, C_CV, C_CZ, C_G = 3240, 3752, 3880, 4008, 4520
V_NG, V_GCQ, V_GCKV, V_BF, V_SK, V_FG, NV = 0, 1024, 1408, 1664, 1672, 1680, 2704
M_E, M_O, M_E0, M_P, M_SP, M_SE, M_SE0, M_TRI, M_E0F, M_PREV, M_PREVP, NMASK = 0, 1, 2, 3, 4, 5, 6, 7, 8, 9, 10, 11


class Prog:
    def __init__(self, nc, stack):
        self.nc = nc
        self.stack = stack
        self.eng = {"pe": nc.tensor, "act": nc.scalar, "dve": nc.vector, "pool": nc.gpsimd, "sp": nc.sync}
        self.sems = {}
        self.cnt = {}
        for e in ("pe", "act", "dve", "pool"):
            self.sems[e] = stack.enter_context(nc.semaphore("s_" + e))
            self.cnt[e] = 0
        self.seen = {e: {} for e in self.eng}
        self.lastw = {}
        self.readers = {}
        self.n_inst = 0

    def dma_sem(self, name):
        if name not in self.sems:
            self.sems[name] = self.stack.enter_context(self.nc.semaphore("d_" + name))
            self.cnt[name] = 0
        return name

    def _deps(self, reads, writes):
        deps = {}

        def add(d):
            if d is not None and d[1] > deps.get(d[0], 0):
                deps[d[0]] = d[1]
        for k in reads:
            add(self.lastw.get(k))
        for k in writes:
            add(self.lastw.get(k))
            for r in self.readers.get(k, ()):
                add(r)
        return deps

    def _wait(self, e, deps):
        for s, v in deps.items():
            if s == "pe" and e == "pe":
                continue
            if self.seen[e].get(s, 0) < v:
                self.eng[e].wait_ge(self.sems[s], v)
                self.seen[e][s] = v

    def _record(self, reads, writes, tok):
        for k in reads:
            lst = self.readers.setdefault(k, [])
            lst[:] = [r for r in lst if r[0] != tok[0]]
            lst.append(tok)
        for k in writes:
            self.lastw[k] = tok
            self.readers[k] = []

    def op(self, e, fn, reads=(), writes=(), inc=True):
        ps = [k for k in reads if k.startswith("pf") or k.startswith("tp")]
        if ps:
            reads = [k for k in reads if k not in ps]
            writes = list(writes) + ps
        self._wait(e, self._deps(reads, writes))
        ins = fn(self.eng[e])
        self.n_inst += 1
        self._record(reads, writes, (e, self.cnt[e] + 1))
        if inc:
            ins.then_inc(self.sems[e], 1)
            self.cnt[e] += 1
        return ins

    def dma(self, q, out, in_, reads=(), writes=(), slot=None, **kw):
        s = self.dma_sem(slot)
        self._wait(q, self._deps(reads, writes))
        ins = self.eng[q].dma_start(out=out, in_=in_, **kw)
        ins.then_inc(self.sems[s], 16)
        self.cnt[s] += 16
        self._record(reads, writes, (s, self.cnt[s]))
        self.n_inst += 1
        return ins

    def barrier(self, engines=("pe", "act", "dve", "pool", "sp")):
        deps = {s: c for s, c in self.cnt.items() if c > 0}
        for e in engines:
            self._wait(e, dict(deps))


class Rot:
    def __init__(self, items):
        self.items = list(items)
        self.i = 0

    def next(self):
        it = self.items[self.i % len(self.items)]
        self.i += 1
        return it


class Ctx:
    pass


def kkey(name, *a):
    return name + "_" + "_".join(str(x) for x in a)


PFK = [f"pf{i}" for i in range(6)]
TPK = ["tp0", "tp1"]


def qslots(C, ci):
    return list(range(4 * ci, min(4 * ci + 4, C.ns)))


def nqc(C):
    return (C.ns + 3) // 4


NKC = (NB + 3) // 4


def rstd_from_ss(C, ss_ap, out_ap_, n, keys):
    P, G = C.P, C.G
    P.op("act", lambda e: e.activation(out=out_ap_, in_=ss_ap, func=AF.Ln, scale=1.0 / n, bias=G["epsb"][:, 0:1]), reads=keys + ["epsb"], writes=keys)
    P.op("act", lambda e: e.activation(out=out_ap_, in_=out_ap_, func=AF.Exp, scale=-0.5), reads=keys, writes=keys)


def norm_block(C, src_rows, i, hnT_dst, hnT_key, gcol, keep_x=False):
    P, G = C.P, C.G
    b2 = i % 2
    xt, xk = G["xt"][b2], f"xt{b2}"
    P.dma("sp", xt[:], src_rows, writes=[xk], slot=xk)
    hb, hbk = G["hnb"][b2], f"hnb{b2}"
    P.op("act", lambda e: e.activation(out=hb[:], in_=xt[:], func=AF.Square, accum_out=G["ss"][:, b2:b2 + 1]), reads=[xk], writes=[hbk, f"ss{b2}"])
    rstd_from_ss(C, G["ss"][:, b2:b2 + 1], G["rs"][:, b2:b2 + 1], float(D), [f"ss{b2}", f"rs{b2}"])
    P.op("dve", lambda e: e.scalar_tensor_tensor(out=hb[:], in0=xt[:], scalar=G["rs"][:, b2:b2 + 1], in1=G["vec"][:, gcol:gcol + D], op0=ALU.mult, op1=ALU.mult),
         reads=[xk, f"rs{b2}", "vec"], writes=[hbk])
    tpb = G["tp"][b2]
    for kc in range(8):
        P.op("pe", lambda e, kc=kc: e.transpose(tpb[:, kc * 128:(kc + 1) * 128], hb[:, kc * 128:(kc + 1) * 128], G["ident_bf"][:]),
             reads=[hbk, "ident_bf"], writes=[TPK[b2]], inc=(kc == 7))
    src = tpb[:, 0:1024].rearrange("p (k t) -> p k t", k=8)
    if i % 2 == 0:
        P.op("act", lambda e: e.activation(out=hnT_dst, in_=src, func=AF.Copy), reads=[TPK[b2]], writes=[hnT_key])
    else:
        P.op("dve", lambda e: e.tensor_copy(out=hnT_dst, in_=src), reads=[TPK[b2]], writes=[hnT_key])


def rope_tm(C, src, dst, cos, sin, nh, half, rkeys, wkeys):
    P, G = C.P, C.G
    t = G["rtmp"]
    n = nh * half
    a = t[:, 0:n].rearrange("p (h d) -> p h d", h=nh)
    b = t[:, 256:256 + n].rearrange("p (h d) -> p h d", h=nh)
    cb = cos.unsqueeze(1).to_broadcast([128, nh, half])
    sbb = sin.unsqueeze(1).to_broadcast([128, nh, half])
    x1 = src[:, :, 0:half]
    x2 = src[:, :, half:2 * half]
    P.op("dve", lambda e: e.tensor_tensor(out=a, in0=x1, in1=cb, op=ALU.mult), reads=rkeys, writes=["rta"])
    P.op("dve", lambda e: e.tensor_tensor(out=b, in0=x2, in1=sbb, op=ALU.mult), reads=rkeys, writes=["rtb"])
    P.op("dve", lambda e: e.tensor_tensor(out=dst[:, :, 0:half], in0=a, in1=b, op=ALU.subtract), reads=["rta", "rtb"], writes=wkeys)
    P.op("dve", lambda e: e.tensor_tensor(out=a, in0=x2, in1=cb, op=ALU.mult), reads=rkeys, writes=["rta"])
    P.op("dve", lambda e: e.tensor_tensor(out=b, in0=x1, in1=sbb, op=ALU.mult), reads=rkeys, writes=["rtb"])
    P.op("dve", lambda e: e.tensor_tensor(out=dst[:, :, half:2 * half], in0=a, in1=b, op=ALU.add), reads=["rta", "rtb"], writes=wkeys)


def softplus_neg(C, u_ps, dst, rkeys, dkey):
    P, G = C.P, C.G
    P.op("dve", lambda e: e.tensor_tensor(out=dst, in0=u_ps, in1=G["vec"][:, V_BF:V_BF + 8], op=ALU.add), reads=rkeys + ["vec"], writes=[dkey])
    P.op("act", lambda e: e.activation(out=dst, in_=dst, func=AF.Exp, scale=-1.0), reads=[dkey], writes=[dkey])
    P.op("act", lambda e: e.activation(out=dst, in_=dst, func=AF.Ln, bias=G["oneb"][:, 0:1]), reads=[dkey, "oneb"], writes=[dkey])


def hn_store(C, scratch, ci, nt, buf, keys, tag):
    C.P.dma("sp", scratch.rearrange("k p t -> p k t")[:, :, ci * 512:ci * 512 + nt], buf[:, :, 0:nt], reads=keys, writes=[f"{tag}{ci}"], slot="hnw" + keys[0][:4])


def hn_load(C, scratch, ci, nt, buf, keys, tag):
    C.P.dma("sp", buf[:, :, 0:nt], scratch.rearrange("k p t -> p k t")[:, :, ci * 512:ci * 512 + nt], reads=[f"{tag}{ci}"], writes=keys, slot="hnl" + keys[0][:4])


def q_hn_chunk(C, ci, sl, buf, bufk, hmine, first):
    nt = len(sl) * 128
    keys = [kkey(bufk, bi) for bi in range(len(sl))]
    if C.full:
        hn_load(C, C.hnA, ci, nt, buf, keys, "hnA")
    elif first:
        for j in sl:
            bi = j - sl[0]
            norm_block(C, hmine[j], j, buf[:, :, bi * 128:(bi + 1) * 128], kkey(bufk, bi), V_NG)
        hn_store(C, C.hnM, ci, nt, buf, keys, "hnM")
    else:
        hn_load(C, C.hnM, ci, nt, buf, keys, "hnM")
    return keys


def load_w(C, dst, key, src):
    C.P.dma("pool", dst, src, writes=[key], slot="w_" + key)


def evac(C, i, dst, src, rk, wk):
    if i % 2 == 0:
        C.P.op("act", lambda e: e.activation(out=dst, in_=src, func=AF.Copy), reads=rk, writes=wk)
    else:
        C.P.op("dve", lambda e: e.tensor_copy(out=dst, in_=src), reads=rk, writes=wk)


def zgate(C, ci, hb, hkeys, wz, wzk, br):
    P, G = C.P, C.G
    pf = G["pf"]
    yg = G["yg"]
    nt = len(qslots(C, ci)) * 128
    for p4 in range(4):
        ps = pf[5]
        for kc in range(8):
            P.op("pe", lambda e, p4=p4, kc=kc: e.matmul(ps[:, 0:nt], lhsT=wz[:, kc, p4 * 128:(p4 + 1) * 128], rhs=hb[:, kc, 0:nt], start=(kc == 0), stop=(kc == 7)),
                 reads=hkeys + [wzk], writes=[PFK[5]], inc=(kc == 7))
        sz = G["sz"]
        P.op("act", lambda e: e.activation(out=sz[:, 0:nt], in_=ps[:, 0:nt], func=AF.Silu), reads=[PFK[5]], writes=["sz"])
        P.op("dve", lambda e, p4=p4: e.tensor_tensor(out=yg[:, p4, 0:nt], in0=yg[:, p4, 0:nt], in1=sz[:, 0:nt], op=ALU.mult), reads=["sz", "yg"], writes=["yg"])
    P.dma("sp", C.ybr[br].rearrange("k p t -> p k t")[:, :, ci * 512:ci * 512 + nt], yg[:, :, 0:nt], reads=["yg"], writes=["ybr_dram%d_%d" % (br, ci)], slot="ybr")


def normalize(C, O, Ok, c_lo, c_hi, dsts, extra_den=None):
    P, G = C.P, C.G
    pf = G["pf"]
    rd = G["rden"]
    ones_f = G["cst"][:, 256:384]
    if extra_den is not None:
        P.op("dve", lambda e: e.tensor_tensor(out=rd[64:65, c_lo:c_hi], in0=O[64:65, c_lo:c_hi], in1=extra_den, op=ALU.add), reads=[Ok, "ES"], writes=["rden"])
        P.op("dve", lambda e: e.reciprocal(out=rd[64:65, c_lo:c_hi], in_=rd[64:65, c_lo:c_hi]), reads=["rden"], writes=["rden"])
    else:
        P.op("dve", lambda e: e.reciprocal(out=rd[64:65, c_lo:c_hi], in_=O[64:65, c_lo:c_hi]), reads=[Ok], writes=["rden"])
    bc = pf[5]
    P.op("pe", lambda e: e.matmul(bc[0:64, c_lo:c_hi], lhsT=ones_f[64:65, 0:64], rhs=rd[64:65, c_lo:c_hi], start=True, stop=True), reads=["rden", "cst"], writes=[PFK[5]])
    bcs = G["bcs"]
    P.op("act", lambda e: e.activation(out=bcs[0:64, c_lo:c_hi], in_=bc[0:64, c_lo:c_hi], func=AF.Copy), reads=[PFK[5]], writes=["bcs"])
    for (lo, hi, dst) in dsts:
        P.op("dve", lambda e, lo=lo, hi=hi, dst=dst: e.tensor_tensor(out=dst, in0=O[0:64, lo:hi], in1=bcs[0:64, lo:hi], op=ALU.mult), reads=[Ok, "bcs"], writes=["yg"])


def attn_chunk(C, ci, scale, KT, V, QT, XQ, xk_fn, xr, bias_fn):
    P, G = C.P, C.G
    pf = G["pf"]
    masks, ident_bf = G["masks"], G["ident_bf"]
    ones_f = G["cst"][:, 256:384]
    sl = qslots(C, ci)
    nt = len(sl) * 128
    full = C.full
    kbmax = sl[-1] if full else min(2 * sl[-1] + 1, NB - 1)
    yg = G["yg"]
    steps = [(h, kb) for h in range(8) for kb in range(kbmax + 1)]
    n = len(steps)
    st = {}
    Obank = {}
    deferred = []

    def stageA(i):
        h, kb = steps[i]
        p4, b0 = h // 2, 64 * (h % 2)
        jmin = max(sl[0], kb if full else kb // 2)
        c0 = (jmin - sl[0]) * 128
        si = C.srot.next()
        S, Sk = pf[si], PFK[si]
        ml = []
        for j in sl:
            if j < jmin:
                continue
            if full:
                if kb == j:
                    ml.append((j, M_E0F if j == 0 else M_TRI))
                elif kb == 0:
                    ml.append((j, M_P))
            elif kb == 2 * j:
                ml.append((j, M_E0 if j == 0 else M_E))
            elif kb == 2 * j + 1:
                ml.append((j, M_O))
            elif kb == 0:
                ml.append((j, M_P))
        P.op("pe", lambda e: e.matmul(S[:, c0:nt], lhsT=KT[:, p4, kb * 128:(kb + 1) * 128], rhs=QT[:, h, c0:nt], start=True, stop=False),
             reads=[kkey("KT", p4, kb // 4), "QT"], writes=[Sk], inc=False)
        P.op("pe", lambda e: e.matmul(S[:, c0:nt], lhsT=xk_fn(h, kb), rhs=XQ[:, h, c0:nt], start=False, stop=(len(ml) == 0)),
             reads=["XK", kkey("kropeT", kb // 4), "XQ"], writes=[Sk], inc=(len(ml) == 0))
        for mi, (j, mid) in enumerate(ml):
            jc = (j - sl[0]) * 128
            P.op("pe", lambda e, jc=jc, mid=mid: e.matmul(S[:, jc:jc + 128], lhsT=ident_bf[:], rhs=masks[:, mid, :], start=False, stop=(mi == len(ml) - 1)),
                 reads=["masks", "ident_bf"], writes=[Sk], inc=(mi == len(ml) - 1))
        st[i] = (S, Sk, c0)

    def stageB(i):
        h, kb = steps[i]
        S, Sk, c0 = st[i]
        pi = C.ptrot.next()
        Pt, Ptk = G["Pt"][pi], f"Pt{pi}"
        bias = bias_fn(h, kb)
        if bias is None:
            P.op("act", lambda e: e.activation(out=Pt[:, c0:nt], in_=S[:, c0:nt], func=AF.Exp, scale=scale), reads=[Sk], writes=[Ptk])
        else:
            P.op("act", lambda e: e.activation(out=Pt[:, c0:nt], in_=S[:, c0:nt], func=AF.Exp, scale=scale, bias=bias), reads=[Sk, kkey("Ck", kb)], writes=[Ptk])
        st[i] = (Pt, Ptk, c0)

    def stageC(i, it):
        h, kb = steps[i]
        Pt, Ptk, c0 = st.pop(i)
        if kb == 0:
            while deferred and deferred[0][2] <= h - 2:
                deferred.pop(0)[1]()
            Obank[h] = C.orot.next()
        oi = Obank[h]
        O, Ok = pf[oi], PFK[oi]
        P.op("pe", lambda e: e.matmul(O[0:65, c0:nt], lhsT=V[:, kb, h, 0:65], rhs=Pt[:, c0:nt], start=(kb == 0), stop=(kb == kbmax)),
             reads=[Ptk, kkey("V", kb), "Vones"], writes=[Ok], inc=True)
        if kb == kbmax:
            p4, b0 = h // 2, 64 * (h % 2)
            r2 = h % 2
            rd, rdk = G["rden2"][r2], f"rden{r2}"
            bcs, bck = G["bcs2"][r2], f"bcs{r2}"
            P.op("dve", lambda e: e.reciprocal(out=rd[64:65, 0:nt], in_=O[64:65, 0:nt]), reads=[Ok], writes=[rdk])
            bc = pf[5]

            def t2():
                P.op("pe", lambda e: e.matmul(bc[0:64, 0:nt], lhsT=ones_f[64:65, 0:64], rhs=rd[64:65, 0:nt], start=True, stop=True), reads=[rdk, "cst"], writes=[PFK[5]])
                P.op("act", lambda e: e.activation(out=bcs[0:64, 0:nt], in_=bc[0:64, 0:nt], func=AF.Copy), reads=[PFK[5]], writes=[bck])

            def t4():
                P.op("dve", lambda e: e.tensor_tensor(out=yg[b0:b0 + 64, p4, 0:nt], in0=O[0:64, 0:nt], in1=bcs[0:64, 0:nt], op=ALU.mult), reads=[Ok, bck], writes=["yg"])
            deferred.append((it + 8, t2, h))
            deferred.append((it + 10, t4, h))

    for it in range(n + 2):
        if it < n:
            stageA(it)
        if 0 <= it - 1 < n:
            stageB(it - 1)
        if 0 <= it - 2 < n:
            stageC(it - 2, it)
        while deferred and deferred[0][0] <= it:
            deferred.pop(0)[1]()
    while deferred:
        deferred.pop(0)[1]()


def phase_fox(C, li, hfull, hmine):
    P, G, nc = C.P, C.G, C.nc
    pf, tp, vec = G["pf"], G["tp"], G["vec"]
    tri_f, ones_f, ident_f = G["cst"][:, 128:256], G["cst"][:, 256:384], G["cst"][:, 0:128]
    ident_bf = G["ident_bf"]
    wv = C.w_in[li].rearrange("(kc p) n -> p kc n", p=128)
    P.barrier()
    with ExitStack() as ps:
        sb = lambda name, shape, dt: ps.enter_context(nc.sbuf_tensor(C.pfx + name, shape, dt))
        KT = sb("f_KT", [128, 4, TK], BF16)
        V = sb("f_V", [128, NB, 8, 65], BF16)
        WKV = sb("f_WKV", [128, 8, 1032], BF16)
        WQ = sb("f_WQ", [128, 8, 520], BF16)
        WZ = sb("f_WZ", [128, 8, 512], BF16)
        Ck = sb("f_Ck", [128, NB, 8], F32)
        tot = sb("f_tot", [128, NB + 1, 8], F32)
        Xf = sb("f_Xf", [128, 512], F32)
        cq = sb("f_cq", [128, 48], F32)
        cqb = sb("f_cqb", [128, 16], BF16)
        spt = sb("f_sp", [128, 8], F32)
        load_w(C, WKV[:, :, 512:1032], "WKVb", wv[:, :, C_AV:C_AV + 520])
        load_w(C, WKV[:, :, 0:512], "WKVa", wv[:, :, C_AK:C_AK + 512])
        load_w(C, WQ[:, :, 0:512], "WQa", wv[:, :, C_AQ:C_AQ + 512])
        load_w(C, WQ[:, :, 512:520], "WQf", wv[:, :, C_AF:C_AF + 8])
        load_w(C, WZ[:, :, :], "WZ", wv[:, :, C_AZ:C_AZ + 512])
        P.op("pool", lambda e: e.memset(tot[:, 0, :], 0.0), writes=["tot"])
        P.op("pool", lambda e: e.memset(V[:, :, :, 64:65], 1.0), writes=["Vones"])
        P.op("pool", lambda e: e.memset(Xf[:], 0.0), writes=["Xf"])
        P.op("pool", lambda e: e.memset(G["XQ"][:], 0.0), writes=["XQ"])
        for ci in range(NKC):
            blks = list(range(4 * ci, min(4 * ci + 4, NB)))
            hnT, hk = G["hnT"][ci % 2], f"hnT{ci % 2}"
            for b in blks:
                bi = b - 4 * ci
                norm_block(C, hfull[b], b, hnT[:, :, bi * 128:(bi + 1) * 128], kkey(hk, bi), V_NG)
                t1, t2 = pf[0], pf[1]
                for (pt, pk, c0, n) in ((t1, PFK[0], 512, 512), (t2, PFK[1], 1024, 8)):
                    for kc in range(8):
                        P.op("pe", lambda e, pt=pt, c0=c0, n=n, kc=kc: e.matmul(pt[:, 0:n], lhsT=hnT[:, kc, bi * 128:(bi + 1) * 128], rhs=WKV[:, kc, c0:c0 + n], start=(kc == 0), stop=(kc == 7)),
                             reads=[kkey(hk, bi), "WKVb"], writes=[pk], inc=(kc == 7))
                evac(C, b, V[:, b, :, 0:64], t1[:, 0:512].rearrange("p (h d) -> p h d", h=8), [PFK[0]], [kkey("V", b)])
                softplus_neg(C, t2[:, 0:8], spt[:, 0:8], [PFK[1]], "sp")
                pc = pf[3]
                P.op("pe", lambda e: e.matmul(pc[:, 0:8], lhsT=tri_f, rhs=spt[:, 0:8], start=True, stop=True), reads=["sp", "cst"], writes=[PFK[3]], inc=False)
                P.op("pe", lambda e: e.matmul(pc[:, 8:16], lhsT=ones_f, rhs=spt[:, 0:8], start=True, stop=True), reads=["sp", "cst"], writes=[PFK[3]])
                P.op("dve", lambda e, b=b: e.tensor_tensor(out=Ck[:, b, :], in0=pc[:, 0:8], in1=tot[:, b, :], op=ALU.add), reads=[PFK[3], "tot"], writes=[kkey("Ck", b)])
                P.op("dve", lambda e, b=b: e.tensor_tensor(out=tot[:, b + 1, :], in0=pc[:, 8:16], in1=tot[:, b, :], op=ALU.add), reads=[PFK[3], "tot"], writes=["tot"])
            nt = len(blks) * 128
            hn_store(C, C.hnA, ci, nt, hnT, [kkey(hk, bi) for bi in range(len(blks))], "hnA")
            for p4 in range(4):
                psm, psk = pf[4 + p4 % 2], PFK[4 + p4 % 2]
                for kc in range(8):
                    P.op("pe", lambda e, p4=p4, kc=kc: e.matmul(psm[:, 0:nt], lhsT=WKV[:, kc, p4 * 128:(p4 + 1) * 128], rhs=hnT[:, kc, 0:nt], start=(kc == 0), stop=(kc == 7)),
                         reads=[kkey(hk, bi) for bi in range(len(blks))] + ["WKVa"], writes=[psk], inc=(kc == 7))
                evac(C, p4, KT[:, p4, ci * 512:ci * 512 + nt], psm[:, 0:nt], [psk], [kkey("KT", p4, ci)])
        QT, XQ = G["QT"], G["XQ"]
        ones_bf = G["ones_bf"]
        for ci in range(nqc(C)):
            sl = qslots(C, ci)
            nt = len(sl) * 128
            buf, bufk = G["hnT"][ci % 2], f"hnT{ci % 2}"
            hkeys = q_hn_chunk(C, ci, sl, buf, bufk, hmine, True)
            for p4 in range(4):
                psm, psk = pf[4 + p4 % 2], PFK[4 + p4 % 2]
                for kc in range(8):
                    P.op("pe", lambda e, p4=p4, kc=kc: e.matmul(psm[:, 0:nt], lhsT=WQ[:, kc, p4 * 128:(p4 + 1) * 128], rhs=buf[:, kc, 0:nt], start=(kc == 0), stop=(kc == 7)),
                         reads=hkeys + ["WQa"], writes=[psk], inc=(kc == 7))
                P.op("act", lambda e, p4=p4: e.activation(out=QT[0:64, 2 * p4, 0:nt], in_=psm[0:64, 0:nt], func=AF.Copy), reads=[psk], writes=["QT"])
                P.op("dve", lambda e, p4=p4: e.tensor_copy(out=QT[64:128, 2 * p4 + 1, 0:nt], in_=psm[64:128, 0:nt]), reads=[psk], writes=["QT"])
            for j in sl:
                bi = j - sl[0]
                t2 = pf[1]
                for kc in range(8):
                    P.op("pe", lambda e, kc=kc: e.matmul(t2[:, 0:8], lhsT=buf[:, kc, bi * 128:(bi + 1) * 128], rhs=WQ[:, kc, 512:520], start=(kc == 0), stop=(kc == 7)),
                         reads=hkeys + ["WQf"], writes=[PFK[1]], inc=(kc == 7))
                softplus_neg(C, t2[:, 0:8], spt[:, 0:8], [PFK[1]], "sp")
                pc = pf[3]
                P.op("pe", lambda e: e.matmul(pc[:, 0:8], lhsT=tri_f, rhs=spt[:, 0:8], start=True, stop=True), reads=["sp", "cst"], writes=[PFK[3]])
                if C.full:
                    P.op("dve", lambda e, j=j: e.tensor_tensor(out=cq[:, 0:8], in0=pc[:, 0:8], in1=tot[:, j, :], op=ALU.add), reads=[PFK[3], "tot"], writes=["cq0"])
                else:
                    P.op("dve", lambda e, j=j: e.tensor_tensor(out=cq[:, 0:8], in0=tot[:, 2 * j + 1, :], in1=tot[:, 2 * j, :], op=ALU.subtract), reads=["tot"], writes=["cq0"])
                    P.op("dve", lambda e, j=j: e.scalar_tensor_tensor(out=cq[:, 0:8], in0=cq[:, 0:8], scalar=G["par"][:, 0:1], in1=tot[:, 2 * j, :], op0=ALU.mult, op1=ALU.add),
                         reads=["cq0", "par", "tot"], writes=["cq0"])
                    P.op("dve", lambda e: e.tensor_tensor(out=cq[:, 0:8], in0=pc[:, 0:8], in1=cq[:, 0:8], op=ALU.add), reads=[PFK[3], "cq0"], writes=["cq0"])
                P.op("dve", lambda e: e.tensor_scalar(out=cq[:, 8:16], in0=cq[:, 0:8], scalar1=-8.0, scalar2=None, op0=ALU.mult), reads=["cq0"], writes=["cq1"])
                P.op("dve", lambda e: e.tensor_copy(out=cqb[:, 0:8], in_=cq[:, 8:16]), reads=["cq1"], writes=["cqb0"])
                P.op("dve", lambda e: e.tensor_copy(out=cq[:, 16:24], in_=cqb[:, 0:8]), reads=["cqb0"], writes=["cq2"])
                P.op("dve", lambda e: e.tensor_tensor(out=cq[:, 24:32], in0=cq[:, 8:16], in1=cq[:, 16:24], op=ALU.subtract), reads=["cq1", "cq2"], writes=["cq3"])
                P.op("dve", lambda e: e.tensor_copy(out=cqb[:, 8:16], in_=cq[:, 24:32]), reads=["cq3"], writes=["cqb1"])
                P.op("dve", lambda e: e.tensor_copy(out=cq[:, 32:40], in_=cqb[:, 8:16]), reads=["cqb1"], writes=["cq4"])
                P.op("dve", lambda e: e.tensor_tensor(out=cq[:, 40:48], in0=cq[:, 24:32], in1=cq[:, 32:40], op=ALU.subtract), reads=["cq3", "cq4"], writes=["cq5"])
                Xv = Xf[:, 0:512].rearrange("p (a b c) -> p a b c", a=4, b=2)
                for r, (c0_, ck) in enumerate(((16, "cq2"), (32, "cq4"), (40, "cq5"))):
                    P.op("dve", lambda e, r=r, c0_=c0_: e.tensor_copy(out=Xv[:, :, :, r], in_=cq[:, c0_:c0_ + 8].rearrange("p (a b) -> p a b", a=4)), reads=[ck], writes=["Xf"])
                xp = pf[2]
                for p4 in range(4):
                    P.op("pe", lambda e, p4=p4: e.transpose(xp[:, p4 * 128:(p4 + 1) * 128], Xf[:, p4 * 128:(p4 + 1) * 128], ident_f), reads=["Xf", "cst"], writes=[PFK[2]], inc=(p4 == 3))
                xpv = xp[:, 0:512].rearrange("p (a t) -> p a t", a=4)
                XQv = XQ[:, :, bi * 128:(bi + 1) * 128].rearrange("p (a e) t -> p a e t", e=2)
                P.op("act", lambda e: e.activation(out=XQv[0:3, :, 0, :], in_=xpv[0:3, :, :], func=AF.Copy), reads=[PFK[2]], writes=["XQ"])
                P.op("act", lambda e: e.activation(out=XQv[0:3, :, 1, :], in_=xpv[64:67, :, :], func=AF.Copy), reads=[PFK[2]], writes=["XQ"])
            attn_chunk(C, ci, 0.125, KT, V, QT, XQ, lambda h, kb: G["ones3"][:, 0:128], 3, lambda h, kb: Ck[:, kb, h:h + 1])
            zgate(C, ci, buf, hkeys, WZ, "WZ", 0)
        P.barrier()


def phase_mla(C, li, hfull, hmine):
    P, G, nc = C.P, C.G, C.nc
    pf, tp, vec = G["pf"], G["tp"], G["vec"]
    ident_bf = G["ident_bf"]
    wv = C.w_in[li].rearrange("(kc p) n -> p kc n", p=128)
    P.barrier()
    with ExitStack() as ps:
        sb = lambda name, shape, dt: ps.enter_context(nc.sbuf_tensor(C.pfx + name, shape, dt))
        KT = sb("m_KT", [128, 4, TK], BF16)
        V = sb("m_V", [128, NB, 8, 65], BF16)
        kropeT = sb("m_kropeT", [128, TK], BF16)
        WKV = sb("m_WKV", [128, 8, 288], BF16)
        WQ = sb("m_WQ", [128, 8, 384], BF16)
        WZ = sb("m_WZ", [128, 8, 512], BF16)
        Wk = sb("m_Wk", [128, 2, 512], BF16)
        Wv = sb("m_Wv", [128, 2, 512], BF16)
        Wqn = sb("m_Wqn", [128, 3, 512], BF16)
        Wqr = sb("m_Wqr", [128, 3, 256], BF16)
        ckc = sb("m_ckc", [128, 2, 512], BF16)
        cqnT = sb("m_cqnT", [128, 3, 512], BF16)
        cb = sb("m_cb", [128, 384], BF16)
        Xb = sb("m_Xb", [128, 512], BF16)
        ropeK = sb("m_ropeK", [128, NB, 32], F32)
        P.dma("sp", ropeK[:], C.ropeK32_d, writes=["ropeK"], slot="rk")
        if C.full:
            ropeQ = ropeK
        else:
            ropeQ = sb("m_ropeQ", [128, NS, 32], F32)
            P.dma("sp", ropeQ[:], C.ropeQ32_d, writes=["ropeQ"], slot="rq")
        load_w(C, WKV[:, :, :], "WKVa", wv[:, :, C_BCKV:C_BCKV + 288])
        ukv_v = C.w_ukv[li].rearrange("(kc p) (h t d) -> p kc t h d", p=128, h=8, t=2)
        for kc in range(2):
            load_w(C, Wv[:, kc, :].rearrange("p (h d) -> p h d", h=8), "Wv", ukv_v[:, kc, 1, :, :])
        for kc in range(2):
            load_w(C, Wk[:, kc, :].rearrange("p (h d) -> p h d", h=8), "Wk", ukv_v[:, kc, 0, :, :])
        load_w(C, WQ[:, :, :], "WQa", wv[:, :, C_BCQ:C_BCQ + 384])
        load_w(C, WZ[:, :, :], "WZ", wv[:, :, C_BZ:C_BZ + 512])
        uq_v = C.w_uq[li].rearrange("(kc p) (h d) -> p kc h d", p=128, h=8)
        for kc in range(3):
            load_w(C, Wqn[:, kc, :].rearrange("p (h d) -> p h d", h=8), "Wqn", uq_v[:, kc, :, 0:64])
            load_w(C, Wqr[:, kc, :].rearrange("p (h d) -> p h d", h=8), "Wqr", uq_v[:, kc, :, 64:96])
        P.op("pool", lambda e: e.memset(V[:, :, :, 64:65], 1.0), writes=["Vones"])
        P.op("pool", lambda e: e.memset(cb[:], 0.0), writes=["cb"])
        P.op("pool", lambda e: e.memset(Xb[:], 0.0), writes=["Xb"])
        P.op("pool", lambda e: e.memset(G["XQ"][:], 0.0), writes=["XQ"])
        for ci in range(NKC):
            blks = list(range(4 * ci, min(4 * ci + 4, NB)))
            hnT, hk = G["hnT"][ci % 2], f"hnT{ci % 2}"
            hn_load(C, C.hnA, ci, len(blks) * 128, hnT, [kkey(hk, bi) for bi in range(len(blks))], "hnA")
            for b in blks:
                bi = b - 4 * ci
                t2 = pf[1]
                for kc in range(8):
                    P.op("pe", lambda e, kc=kc: e.matmul(t2[:, 0:288], lhsT=hnT[:, kc, bi * 128:(bi + 1) * 128], rhs=WKV[:, kc, 0:288], start=(kc == 0), stop=(kc == 7)),
                         reads=[kkey(hk, bi), "WKVa"], writes=[PFK[1]], inc=(kc == 7))
                P.op("act", lambda e: e.activation(out=G["sz"][:, 0:256], in_=t2[:, 0:256], func=AF.Square, accum_out=G["ss"][:, 2:3]), reads=[PFK[1]], writes=["sz", "ss2"])
                rstd_from_ss(C, G["ss"][:, 2:3], G["rs"][:, 2:3], 256.0, ["ss2", "rs2"])
                P.op("dve", lambda e: e.scalar_tensor_tensor(out=cb[:, 0:256], in0=t2[:, 0:256], scalar=G["rs"][:, 2:3], in1=vec[:, V_GCKV:V_GCKV + 256], op0=ALU.mult, op1=ALU.mult),
                     reads=[PFK[1], "rs2", "vec"], writes=["cb"])
                rope_tm(C, t2[:, 256:288].rearrange("p (h d) -> p h d", h=1), cb[:, 256:288].rearrange("p (h d) -> p h d", h=1),
                        ropeK[:, b, 0:16], ropeK[:, b, 16:32], 1, 16, [PFK[1], "ropeK"], ["cb"])
                tq, tqk = tp[(b + 1) % 2], TPK[(b + 1) % 2]
                for k3 in range(3):
                    P.op("pe", lambda e, k3=k3: e.transpose(tq[:, k3 * 128:(k3 + 1) * 128], cb[:, k3 * 128:(k3 + 1) * 128], ident_bf[:]), reads=["cb", "ident_bf"], writes=[tqk], inc=(k3 == 2))
                P.op("dve", lambda e, bi=bi: e.tensor_copy(out=ckc[:, :, bi * 128:(bi + 1) * 128], in_=tq[:, 0:256].rearrange("p (k t) -> p k t", k=2)), reads=[tqk], writes=[kkey("ckc", bi)])
                P.op("act", lambda e, b=b: e.activation(out=kropeT[:, b * 128:(b + 1) * 128], in_=tq[:, 256:384], func=AF.Copy), reads=[tqk], writes=[kkey("kropeT", b // 4)])
                pv, pvk = pf[b % 2 * 2], PFK[b % 2 * 2]
                for kc in range(2):
                    P.op("pe", lambda e, kc=kc, bi=bi: e.matmul(pv[:, 0:512], lhsT=ckc[:, kc, bi * 128:(bi + 1) * 128], rhs=Wv[:, kc, :], start=(kc == 0), stop=(kc == 1)),
                         reads=[kkey("ckc", bi), "Wv"], writes=[pvk], inc=(kc == 1))
                evac(C, b, V[:, b, :, 0:64], pv[:, 0:512].rearrange("p (h d) -> p h d", h=8), [pvk], [kkey("V", b)])
            nt = len(blks) * 128
            ckeys = [kkey("ckc", bi) for bi in range(len(blks))]
            for p4 in range(4):
                psm, psk = pf[4 + p4 % 2], PFK[4 + p4 % 2]
                for kc in range(2):
                    P.op("pe", lambda e, p4=p4, kc=kc: e.matmul(psm[:, 0:nt], lhsT=Wk[:, kc, p4 * 128:(p4 + 1) * 128], rhs=ckc[:, kc, 0:nt], start=(kc == 0), stop=(kc == 1)),
                         reads=ckeys + ["Wk"], writes=[psk], inc=(kc == 1))
                evac(C, p4, KT[:, p4, ci * 512:ci * 512 + nt], psm[:, 0:nt], [psk], [kkey("KT", p4, ci)])
        QT, XQ = G["QT"], G["XQ"]
        for ci in range(nqc(C)):
            sl = qslots(C, ci)
            nt = len(sl) * 128
            buf, bufk = G["hnT"][ci % 2], f"hnT{ci % 2}"
            hkeys = q_hn_chunk(C, ci, sl, buf, bufk, hmine, False)
            for j in sl:
                bi = j - sl[0]
                t1 = pf[0]
                for kc in range(8):
                    P.op("pe", lambda e, kc=kc: e.matmul(t1[:, 0:384], lhsT=buf[:, kc, bi * 128:(bi + 1) * 128], rhs=WQ[:, kc, 0:384], start=(kc == 0), stop=(kc == 7)),
                         reads=hkeys + ["WQa"], writes=[PFK[0]], inc=(kc == 7))
                P.op("act", lambda e: e.activation(out=G["sz"][:, 0:384], in_=t1[:, 0:384], func=AF.Square, accum_out=G["ss"][:, 2:3]), reads=[PFK[0]], writes=["sz", "ss2"])
                rstd_from_ss(C, G["ss"][:, 2:3], G["rs"][:, 2:3], 384.0, ["ss2", "rs2"])
                P.op("dve", lambda e: e.scalar_tensor_tensor(out=cb[:, 0:384], in0=t1[:, 0:384], scalar=G["rs"][:, 2:3], in1=vec[:, V_GCQ:V_GCQ + 384], op0=ALU.mult, op1=ALU.mult),
                     reads=[PFK[0], "rs2", "vec"], writes=["cb"])
                tq, tqk = tp[j % 2], TPK[j % 2]
                for k3 in range(3):
                    P.op("pe", lambda e, k3=k3: e.transpose(tq[:, k3 * 128:(k3 + 1) * 128], cb[:, k3 * 128:(k3 + 1) * 128], ident_bf[:]), reads=["cb", "ident_bf"], writes=[tqk], inc=(k3 == 2))
                P.op("dve", lambda e, bi=bi: e.tensor_copy(out=cqnT[:, :, bi * 128:(bi + 1) * 128], in_=tq[:, 0:384].rearrange("p (k t) -> p k t", k=3)), reads=[tqk], writes=[kkey("cqnT", bi)])
                t2 = pf[1]
                for kc in range(3):
                    P.op("pe", lambda e, kc=kc, bi=bi: e.matmul(t2[:, 0:256], lhsT=cqnT[:, kc, bi * 128:(bi + 1) * 128], rhs=Wqr[:, kc, 0:256], start=(kc == 0), stop=(kc == 2)),
                         reads=[kkey("cqnT", bi), "Wqr"], writes=[PFK[1]], inc=(kc == 2))
                Xbv = Xb[:, 0:512].rearrange("p (a b c) -> p a b c", a=4, b=2)[:, :, :, 0:32].rearrange("p a b c -> p (a b) c")
                rope_tm(C, t2[:, 0:256].rearrange("p (h d) -> p h d", h=8), Xbv, ropeQ[:, j, 0:16], ropeQ[:, j, 16:32], 8, 16, [PFK[1], "ropeQ", "ropeK"], ["Xb"])
                tq2, tq2k = tp[(j + 1) % 2], TPK[(j + 1) % 2]
                for p4 in range(4):
                    P.op("pe", lambda e, p4=p4: e.transpose(tq2[:, p4 * 128:(p4 + 1) * 128], Xb[:, p4 * 128:(p4 + 1) * 128], ident_bf[:]), reads=["Xb", "ident_bf"], writes=[tq2k], inc=(p4 == 3))
                tqv = tq2[:, 0:512].rearrange("p (a t) -> p a t", a=4)
                XQv = XQ[:, :, bi * 128:(bi + 1) * 128].rearrange("p (a e) t -> p a e t", e=2)
                P.op("act", lambda e: e.activation(out=XQv[0:32, :, 0, :], in_=tqv[0:32, :, :], func=AF.Copy), reads=[tq2k], writes=["XQ"])
                P.op("act", lambda e: e.activation(out=XQv[0:32, :, 1, :], in_=tqv[64:96, :, :], func=AF.Copy), reads=[tq2k], writes=["XQ"])
            ckeys = [kkey("cqnT", bi) for bi in range(len(sl))]
            for p4 in range(4):
                psm, psk = pf[4 + p4 % 2], PFK[4 + p4 % 2]
                for kc in range(3):
                    P.op("pe", lambda e, p4=p4, kc=kc: e.matmul(psm[:, 0:nt], lhsT=Wqn[:, kc, p4 * 128:(p4 + 1) * 128], rhs=cqnT[:, kc, 0:nt], start=(kc == 0), stop=(kc == 2)),
                         reads=ckeys + ["Wqn"], writes=[psk], inc=(kc == 2))
                P.op("act", lambda e, p4=p4: e.activation(out=QT[0:64, 2 * p4, 0:nt], in_=psm[0:64, 0:nt], func=AF.Copy), reads=[psk], writes=["QT"])
                P.op("dve", lambda e, p4=p4: e.tensor_copy(out=QT[64:128, 2 * p4 + 1, 0:nt], in_=psm[64:128, 0:nt]), reads=[psk], writes=["QT"])
            attn_chunk(C, ci, 96.0 ** -0.5, KT, V, QT, XQ, lambda h, kb: kropeT[:, kb * 128:(kb + 1) * 128], 32, lambda h, kb: None)
            zgate(C, ci, buf, hkeys, WZ, "WZ", 1)
        P.barrier()


def phase_swa(C, li, hfull, hmine):
    P, G, nc = C.P, C.G, C.nc
    pf, tp, vec = G["pf"], G["tp"], G["vec"]
    ident_bf, masks = G["ident_bf"], G["masks"]
    wv = C.w_in[li].rearrange("(kc p) n -> p kc n", p=128)
    P.barrier()
    with ExitStack() as ps:
        sb = lambda name, shape, dt: ps.enter_context(nc.sbuf_tensor(C.pfx + name, shape, dt))
        KTC = sb("s_KTC", [128, 2, TK], BF16)
        VC = sb("s_VC", [128, NB, 2, 65], BF16)
        WKV = sb("s_WKV", [128, 8, 256], BF16)
        WQ = sb("s_WQ", [128, 8, 512], BF16)
        WZ = sb("s_WZ", [128, 8, 512], BF16)
        cb = sb("s_cb", [128, 256], BF16)
        Xb = sb("s_Xb", [128, 512], BF16)
        es = sb("s_es", [128, 8], F32)
        ES = sb("s_ES", [128, 2, 512], F32)
        QTs_t = sb("s_QT", [128, 4096], BF16)
        QTs = QTs_t[:, 0:4096].rearrange("p (s h t) -> p s h t", s=4, h=8)
        masks4 = sb("s_m4", [128, NMASK, 512], BF16)
        ropeK = sb("s_ropeK", [128, NB, 64], F32)
        P.dma("sp", ropeK[:], C.ropeK64_d, writes=["ropeK"], slot="rk")
        if C.full:
            ropeQ = ropeK
        else:
            ropeQ = sb("s_ropeQ", [128, NS, 64], F32)
            P.dma("sp", ropeQ[:], C.ropeQ64_d, writes=["ropeQ"], slot="rq")
        load_w(C, WKV[:, :, :], "WKVa", wv[:, :, C_CK:C_CK + 256])
        load_w(C, WQ[:, :, :], "WQa", wv[:, :, C_CQ:C_CQ + 512])
        load_w(C, WZ[:, :, :], "WZ", wv[:, :, C_CZ:C_CZ + 512])
        P.op("pool", lambda e: e.memset(VC[:, :, :, 64:65], 1.0), writes=["Vones"])
        P.op("pool", lambda e: e.memset(QTs_t[:], 0.0), writes=["QTs"])
        for g4 in range(4):
            P.op("dve", lambda e, g4=g4: e.tensor_copy(out=masks4[:, :, g4 * 128:(g4 + 1) * 128], in_=masks[:, :, :]), reads=["masks"], writes=["masks4"])
        P.op("act", lambda e: e.activation(out=es[:, 0:8], in_=vec[:, V_SK:V_SK + 8], func=AF.Exp), reads=["vec"], writes=["es"])
        P.op("pool", lambda e: e.memset(ES[:], 0.0), writes=["ES"])
        for h in range(8):
            P.op("dve", lambda e, h=h: e.tensor_scalar(out=ES[:, h // 4, (h % 4) * 128:(h % 4 + 1) * 128], in0=ES[:, h // 4, (h % 4) * 128:(h % 4 + 1) * 128],
                                                       scalar1=es[:, h:h + 1], scalar2=None, op0=ALU.add), reads=["es", "ES"], writes=["ES"])
        for ci in range(NKC):
            blks = list(range(4 * ci, min(4 * ci + 4, NB)))
            hnT, hk = G["hnT"][ci % 2], f"hnT{ci % 2}"
            hn_load(C, C.hnA, ci, len(blks) * 128, hnT, [kkey(hk, bi) for bi in range(len(blks))], "hnA")
            for b in blks:
                bi = b - 4 * ci
                t3 = pf[2]
                for kc in range(8):
                    P.op("pe", lambda e, kc=kc: e.matmul(t3[:, 0:256], lhsT=hnT[:, kc, bi * 128:(bi + 1) * 128], rhs=WKV[:, kc, 0:256], start=(kc == 0), stop=(kc == 7)),
                         reads=[kkey(hk, bi), "WKVa"], writes=[PFK[2]], inc=(kc == 7))
                P.op("act", lambda e, b=b: e.activation(out=VC[:, b, :, 0:64], in_=t3[:, 128:256].rearrange("p (h d) -> p h d", h=2), func=AF.Copy), reads=[PFK[2]], writes=[kkey("VC", b)])
                kd = cb[:, 0:256].rearrange("p (h t d) -> p h t d", h=2, t=2)
                rope_tm(C, t3[:, 0:128].rearrange("p (h d) -> p h d", h=2), kd[:, :, 0, :], ropeK[:, b, 0:32], ropeK[:, b, 32:64], 2, 32, [PFK[2], "ropeK"], ["cb"])
                P.op("dve", lambda e: e.tensor_copy(out=kd[:, :, 1, :], in_=kd[:, :, 0, :]), reads=["cb"], writes=["cb"])
                tq, tqk = tp[(b + 1) % 2], TPK[(b + 1) % 2]
                for k2 in range(2):
                    P.op("pe", lambda e, k2=k2: e.transpose(tq[:, k2 * 128:(k2 + 1) * 128], cb[:, k2 * 128:(k2 + 1) * 128], ident_bf[:]), reads=["cb", "ident_bf"], writes=[tqk], inc=(k2 == 1))
                P.op("act", lambda e, b=b: e.activation(out=KTC[:, :, b * 128:(b + 1) * 128], in_=tq[:, 0:256].rearrange("p (k t) -> p k t", k=2), func=AF.Copy), reads=[tqk], writes=[kkey("KTC", b)])
        ones_f = G["cst"][:, 256:384]
        for ci in range(nqc(C)):
            sl = qslots(C, ci)
            nt = len(sl) * 128
            buf, bufk = G["hnT"][ci % 2], f"hnT{ci % 2}"
            yg = G["yg"]
            hkeys = q_hn_chunk(C, ci, sl, buf, bufk, hmine, False)
            for j in sl:
                bi = j - sl[0]
                t1 = pf[5]
                for kc in range(8):
                    P.op("pe", lambda e, kc=kc: e.matmul(t1[:, 0:512], lhsT=buf[:, kc, bi * 128:(bi + 1) * 128], rhs=WQ[:, kc, 0:512], start=(kc == 0), stop=(kc == 7)),
                         reads=hkeys + ["WQa"], writes=[PFK[5]], inc=(kc == 7))
                rope_tm(C, t1[:, 0:512].rearrange("p (h d) -> p h d", h=8), Xb[:, 0:512].rearrange("p (h d) -> p h d", h=8), ropeQ[:, j, 0:32], ropeQ[:, j, 32:64], 8, 32, [PFK[5], "ropeQ", "ropeK"], ["Xb"])
                tq2, tq2k = tp[j % 2], TPK[j % 2]
                for p4 in range(4):
                    P.op("pe", lambda e, p4=p4: e.transpose(tq2[:, p4 * 128:(p4 + 1) * 128], Xb[:, p4 * 128:(p4 + 1) * 128], ident_bf[:]), reads=["Xb", "ident_bf"], writes=[tq2k], inc=(p4 == 3))
                tqv = tq2[:, 0:512].rearrange("p (a t) -> p a t", a=4)
                Qv = QTs[:, bi, :, :].rearrange("p (a e) t -> p a e t", e=2)
                P.op("act", lambda e: e.activation(out=Qv[0:64, :, 0, :], in_=tqv[0:64, :, :], func=AF.Copy), reads=[tq2k], writes=["QTs"])
                P.op("act", lambda e: e.activation(out=Qv[64:128, :, 1, :], in_=tqv[64:128, :, :], func=AF.Copy), reads=[tq2k], writes=["QTs"])
            steps = []
            for j in sl:
                if C.full:
                    kbs = [kb for kb in (j - 1, j) if 0 <= kb < NB]
                else:
                    kbs = [kb for kb in (2 * j - 1, 2 * j, 2 * j + 1) if 0 <= kb < NB]
                for kvh in range(2):
                    for ki, kb in enumerate(kbs):
                        steps.append((j, kvh, ki, kb, len(kbs)))
            n = len(steps)
            stt = {}
            Obank = {}
            deferred = []

            def stageA(i):
                j, kvh, ki, kb, nk = steps[i]
                bi = j - sl[0]
                si = C.srot.next()
                S, Sk = pf[si], PFK[si]
                if C.full:
                    if kb == j:
                        mid = M_E0F if j == 0 else M_TRI
                    else:
                        mid = M_PREVP if kb == 0 else M_PREV
                elif kb == 2 * j - 1:
                    mid = M_SP
                elif kb == 2 * j:
                    mid = M_SE0 if j == 0 else M_SE
                else:
                    mid = M_O
                P.op("pe", lambda e: e.matmul(S[:, 0:512], lhsT=KTC[:, kvh, kb * 128:(kb + 1) * 128], rhs=QTs[:, bi, 4 * kvh:4 * kvh + 4, :].rearrange("p h t -> p (h t)"), start=True, stop=False),
                     reads=[kkey("KTC", kb), "QTs"], writes=[Sk], inc=False)
                P.op("pe", lambda e, mid=mid: e.matmul(S[:, 0:512], lhsT=ident_bf[:], rhs=masks4[:, mid, :], start=False, stop=True),
                     reads=["masks4", "ident_bf"], writes=[Sk], inc=True)
                stt[i] = (S, Sk)

            def stageB(i):
                S, Sk = stt[i]
                pi = C.ptrot.next()
                Pt, Ptk = G["Pt"][pi], f"Pt{pi}"
                P.op("act", lambda e: e.activation(out=Pt[:, 0:512], in_=S[:, 0:512], func=AF.Exp, scale=0.125), reads=[Sk], writes=[Ptk])
                stt[i] = (Pt, Ptk)

            def stageC(i, it):
                j, kvh, ki, kb, nk = steps[i]
                bi = j - sl[0]
                u = 2 * bi + kvh
                Pt, Ptk = stt.pop(i)
                if ki == 0:
                    while deferred and deferred[0][2] <= u - 2:
                        deferred.pop(0)[1]()
                    Obank[u] = C.orot.next()
                oi = Obank[u]
                O, Ok = pf[oi], PFK[oi]
                P.op("pe", lambda e: e.matmul(O[0:65, 0:512], lhsT=VC[:, kb, kvh, 0:65], rhs=Pt[:, 0:512], start=(ki == 0), stop=(ki == nk - 1)),
                     reads=[Ptk, kkey("VC", kb), "Vones"], writes=[Ok], inc=True)
                if ki == nk - 1:
                    r2 = u % 2
                    rd, rdk = G["rden2"][r2], f"rden{r2}"
                    bcs, bck = G["bcs2"][r2], f"bcs{r2}"
                    P.op("dve", lambda e: e.tensor_tensor(out=rd[64:65, 0:512], in0=O[64:65, 0:512], in1=ES[64:65, kvh, :], op=ALU.add), reads=[Ok, "ES"], writes=[rdk])
                    P.op("dve", lambda e: e.reciprocal(out=rd[64:65, 0:512], in_=rd[64:65, 0:512]), reads=[rdk], writes=[rdk])
                    bc = pf[5]

                    def t2():
                        P.op("pe", lambda e: e.matmul(bc[0:64, 0:512], lhsT=ones_f[64:65, 0:64], rhs=rd[64:65, 0:512], start=True, stop=True), reads=[rdk, "cst"], writes=[PFK[5]])
                        P.op("act", lambda e: e.activation(out=bcs[0:64, 0:512], in_=bc[0:64, 0:512], func=AF.Copy), reads=[PFK[5]], writes=[bck])

                    def t4():
                        for g in range(4):
                            h = 4 * kvh + g
                            b0 = 64 * (h % 2)
                            P.op("dve", lambda e, g=g, h=h, b0=b0: e.tensor_tensor(out=yg[b0:b0 + 64, h // 2, bi * 128:(bi + 1) * 128], in0=O[0:64, g * 128:(g + 1) * 128],
                                                                                   in1=bcs[0:64, g * 128:(g + 1) * 128], op=ALU.mult), reads=[Ok, bck], writes=["yg"])
                    deferred.append((it + 6, t2, u))
                    deferred.append((it + 8, t4, u))

            for it in range(n + 2):
                if it < n:
                    stageA(it)
                if 0 <= it - 1 < n:
                    stageB(it - 1)
                if 0 <= it - 2 < n:
                    stageC(it - 2, it)
                while deferred and deferred[0][0] <= it:
                    deferred.pop(0)[1]()
            while deferred:
                deferred.pop(0)[1]()
            zgate(C, ci, buf, hkeys, WZ, "WZ", 2)
        P.barrier()


def phase_final(C, li, hmine, out_ap, last):
    P, G, nc = C.P, C.G, C.nc
    pf, tp, vec = G["pf"], G["tp"], G["vec"]
    wv = C.w_in[li].rearrange("(kc p) n -> p kc n", p=128)
    P.barrier()
    with ExitStack() as ps:
        sb = lambda name, shape, dt: ps.enter_context(nc.sbuf_tensor(C.pfx + name, shape, dt))
        WG = sb("z_WG", [128, 8, 3072], BF16)
        WB = sb("z_WB", [128, 12, 1024], BF16)
        WO = sb("z_WO", [128, 8, 1024], BF16)
        mT = sb("z_mT", [128, 8, 512], BF16)
        ybuf = sb("z_ybuf", [128, 12, 512], BF16)
        sg = [sb(f"z_sg{i}", [128, 512], F32) for i in range(2)]
        macc = sb("z_macc", [128, 512], F32)
        tmpm = sb("z_tmpm", [128, 512], F32)
        fg = sb("z_fg", [128, D], F32)
        if last:
            P.dma("sp", fg[:], C.fg_d, writes=["fg"], slot="fg")
        wbv = C.w_branch[li].rearrange("n (kc p) d -> p n kc d", p=128)
        FK = [[f"F{q}_{n}{t}" for n in range(3) for t in "gb"] for q in range(4)]
        for q in range(4):
            for n in range(3):
                c0 = n * 1024 + q * 256
                P.dma("pool", WG[:, :, c0:c0 + 256], wv[:, :, C_G + c0:C_G + c0 + 256], writes=[f"F{q}_{n}g"], slot=(f"w_F0_{n}" if q == 0 else f"w_F{q}"))
                P.dma("pool", WB[:, n * 4:(n + 1) * 4, q * 256:(q + 1) * 256], wbv[:, n, :, q * 256:(q + 1) * 256], writes=[f"F{q}_{n}b"], slot=(f"w_F0_{n}" if q == 0 else f"w_F{q}"))
        load_w(C, WO[:, :, :], "WO", C.w_out[li].rearrange("(kc p) n -> p kc n", p=128))
        for ci in range(nqc(C)):
            sl = qslots(C, ci)
            nt = len(sl) * 128
            buf, bufk = G["hnT"][ci % 2], f"hnT{ci % 2}"
            hkeys = q_hn_chunk(C, ci, sl, buf, bufk, hmine, False)
            for n in range(3):
                P.dma("sp", ybuf[:, n * 4:(n + 1) * 4, 0:nt], C.ybr[n].rearrange("k p t -> p k t")[:, :, ci * 512:ci * 512 + nt],
                      reads=["ybr_dram%d_%d" % (n, ci)], writes=["ybuf%d" % n], slot="ybuf%d" % n)
            for dch in range(8):
                for n in range(3):
                    Gp, Gk = pf[n % 3], PFK[n % 3]
                    for kc in range(8):
                        P.op("pe", lambda e, n=n, kc=kc: e.matmul(Gp[:, 0:nt], lhsT=WG[:, kc, n * 1024 + dch * 128:n * 1024 + (dch + 1) * 128], rhs=buf[:, kc, 0:nt], start=(kc == 0), stop=(kc == 7)),
                             reads=hkeys + ([f"F0_{n}g", f"F0_{n}b"] if dch < 2 else FK[dch // 2]), writes=[Gk], inc=(kc == 7))
                    sgn, sgk = sg[n % 2], f"sg{n % 2}"
                    P.op("act", lambda e: e.activation(out=sgn[:, 0:nt], in_=Gp[:, 0:nt], func=AF.Sigmoid), reads=[Gk], writes=[sgk])
                    Pn, Pnk = pf[3 + n % 2], PFK[3 + n % 2]
                    for kc in range(4):
                        P.op("pe", lambda e, n=n, kc=kc: e.matmul(Pn[:, 0:nt], lhsT=WB[:, n * 4 + kc, dch * 128:(dch + 1) * 128], rhs=ybuf[:, n * 4 + kc, 0:nt], start=(kc == 0), stop=(kc == 3)),
                             reads=["ybuf%d" % n] + ([f"F0_{n}g", f"F0_{n}b"] if dch < 2 else FK[dch // 2]), writes=[Pnk], inc=(kc == 3))
                    if n == 0:
                        P.op("dve", lambda e: e.tensor_tensor(out=macc[:, 0:nt], in0=Pn[:, 0:nt], in1=sgn[:, 0:nt], op=ALU.mult), reads=[Pnk, sgk], writes=["macc"])
                    else:
                        P.op("dve", lambda e: e.tensor_tensor(out=tmpm[:, 0:nt], in0=Pn[:, 0:nt], in1=sgn[:, 0:nt], op=ALU.mult), reads=[Pnk, sgk], writes=["tmpm"])
                        if n == 1:
                            P.op("pool", lambda e: e.tensor_tensor(out=macc[:, 0:nt], in0=macc[:, 0:nt], in1=tmpm[:, 0:nt], op=ALU.add), reads=["tmpm", "macc"], writes=["macc"])
                        else:
                            P.op("pool", lambda e, dch=dch: e.tensor_tensor(out=mT[:, dch, 0:nt], in0=macc[:, 0:nt], in1=tmpm[:, 0:nt], op=ALU.add), reads=["tmpm", "macc"], writes=[kkey("mT", dch)])
            mkeys = [kkey("mT", dch) for dch in range(8)]
            for j in sl:
                bi = j - sl[0]
                xk_t, xkk = G["xt"][j % 2], f"xt{j % 2}"
                P.dma("sp", xk_t[:], hmine[j], writes=[xkk], slot=xkk)
                for hlf in range(2):
                    Ob, Obk = pf[hlf], PFK[hlf]
                    for dch in range(8):
                        P.op("pe", lambda e, dch=dch, hlf=hlf: e.matmul(Ob[:, 0:512], lhsT=mT[:, dch, bi * 128:(bi + 1) * 128], rhs=WO[:, dch, hlf * 512:(hlf + 1) * 512], start=(dch == 0), stop=(dch == 7)),
                             reads=mkeys + ["WO"], writes=[Obk], inc=(dch == 7))
                    P.op("dve", lambda e, hlf=hlf: e.tensor_tensor(out=xk_t[:, hlf * 512:(hlf + 1) * 512], in0=Ob[:, 0:512], in1=xk_t[:, hlf * 512:(hlf + 1) * 512], op=ALU.add),
                         reads=[Obk, xkk], writes=[xkk])
                if last:
                    hb, hbk = G["hnb"][j % 2], f"hnb{j % 2}"
                    P.op("act", lambda e: e.activation(out=hb[:], in_=xk_t[:], func=AF.Square, accum_out=G["ss"][:, 3:4]), reads=[xkk], writes=[hbk, "ss3"])
                    rstd_from_ss(C, G["ss"][:, 3:4], G["rs"][:, 3:4], float(D), ["ss3", "rs3"])
                    P.op("dve", lambda e: e.scalar_tensor_tensor(out=xk_t[:], in0=xk_t[:], scalar=G["rs"][:, 3:4], in1=fg[:], op0=ALU.mult, op1=ALU.mult),
                         reads=[xkk, "rs3", "fg"], writes=[xkk])
                P.dma("sp", out_ap[j], xk_t[:], reads=[xkk], writes=["out_dram"], slot=f"outw{j % 2}")
        P.barrier()


def build_program(n_layers_here, is_final, phases=("fox", "mla", "swa", "final"), ybr_kind="Internal"):
    nc = bass.Bass("TRN2", target_bir_lowering=False)
    C = Ctx()
    C.nc = nc
    C.full = False
    C.ns = NS
    C.pfx = ""
    hfull = nc.dram_tensor("hfull", [NB, 128, D], F32, kind="ExternalInput").ap()
    hmine = nc.dram_tensor("hmine", [NS, 128, D], F32, kind="ExternalInput").ap()
    C.w_in = nc.dram_tensor("w_in", [n_layers_here, D, NIN], F32, kind="ExternalInput").ap()
    C.w_uq = nc.dram_tensor("w_uq", [n_layers_here, 384, 768], F32, kind="ExternalInput").ap()
    C.w_ukv = nc.dram_tensor("w_ukv", [n_layers_here, 256, 1024], F32, kind="ExternalInput").ap()
    C.w_branch = nc.dram_tensor("w_branch", [n_layers_here, 3, 512, D], F32, kind="ExternalInput").ap()
    C.w_out = nc.dram_tensor("w_out", [n_layers_here, D, D], F32, kind="ExternalInput").ap()
    vec_d = nc.dram_tensor("vecs", [n_layers_here, 128, V_FG], F32, kind="ExternalInput").ap()
    C.fg_d = nc.dram_tensor("fg", [128, D], F32, kind="ExternalInput").ap()
    cst_d = nc.dram_tensor("cst", [128, 384], F32, kind="ExternalInput").ap()
    masks_d = nc.dram_tensor("masks", [128, NMASK, 128], BF16, kind="ExternalInput").ap()
    par_d = nc.dram_tensor("par", [128, 1], F32, kind="ExternalInput").ap()
    C.ropeK64_d = nc.dram_tensor("ropeK64", [128, NB, 64], F32, kind="ExternalInput").ap()
    C.ropeK32_d = nc.dram_tensor("ropeK32", [128, NB, 32], F32, kind="ExternalInput").ap()
    C.ropeQ64_d = nc.dram_tensor("ropeQ64", [128, NS, 64], F32, kind="ExternalInput").ap()
    C.ropeQ32_d = nc.dram_tensor("ropeQ32", [128, NS, 32], F32, kind="ExternalInput").ap()
    out_d = nc.dram_tensor("out", [NS, 128, D], F32, kind="ExternalOutput").ap()
    C.hnA = nc.dram_tensor("hnA_s", [8, 128, TK], BF16, kind="Internal").ap()
    C.hnM = nc.dram_tensor("hnM_s", [8, 128, TQ], BF16, kind="Internal").ap()
    C.ybr = nc.dram_tensor("ybr_in" if ybr_kind == "ExternalInput" else "ybr_s", [3, 4, 128, TK], BF16, kind=ybr_kind).ap()

    with ExitStack() as st:
        P = Prog(nc, st)
        C.P = P
        G = {}
        C.G = G
        sb = lambda name, shape, dt: st.enter_context(nc.sbuf_tensor(name, shape, dt))
        G["pf"] = [st.enter_context(nc.psum_tensor(f"pf{i}", [128, 512], F32)) for i in range(6)]
        G["tp"] = [st.enter_context(nc.psum_tensor(f"tp{i}", [128, 1024], BF16)) for i in range(2)]
        G["vec"] = sb("vec", [128, V_FG], F32)
        G["cst"] = sb("cstt", [128, 384], F32)
        G["masks"] = sb("maskt", [128, NMASK, 128], BF16)
        G["ident_bf"] = sb("identb", [128, 128], BF16)
        G["ones_bf"] = sb("onesb", [128, 128], BF16)
        G["par"] = sb("part", [128, 1], F32)
        G["epsb"] = sb("epsb", [128, 1], F32)
        G["oneb"] = sb("oneb", [128, 1], F32)
        G["xt"] = [sb(f"xt{i}", [128, D], F32) for i in range(2)]
        G["hnb"] = [sb(f"hnb{i}", [128, D], BF16) for i in range(2)]
        G["ss"] = sb("ss", [128, 4], F32)
        G["rs"] = sb("rs", [128, 4], F32)
        G["hnT"] = [sb(f"hnT{i}", [128, 8, 512], BF16) for i in range(2)]
        G["rtmp"] = sb("rtmp", [128, 512], F32)
        G["QT"] = sb("QT", [128, 8, 512], BF16)
        G["XQ"] = sb("XQ", [128, 8, 512], BF16)
        G["ones3"] = sb("ones3", [128, 128], BF16)
        G["Pt"] = [sb(f"Pt{i}", [128, 512], BF16) for i in range(3)]
        G["yg"] = sb("yg", [128, 4, 512], BF16)
        G["sz"] = sb("sz", [128, 512], BF16)
        G["rden"] = sb("rden", [128, 512], F32)
        G["bcs"] = sb("bcs", [64, 512], F32)
        G["rden2"] = [sb(f"rden2_{i}", [128, 512], F32) for i in range(2)]
        G["bcs2"] = [sb(f"bcs2_{i}", [64, 512], F32) for i in range(2)]
        C.srot = Rot([0, 1, 2])
        C.orot = Rot([3, 4])
        C.ptrot = Rot([0, 1, 2])

        P.dma("sp", G["cst"][:], cst_d, writes=["cst"], slot="c0")
        P.dma("sp", G["masks"][:], masks_d, writes=["masks"], slot="c1")
        P.dma("sp", G["par"][:], par_d, writes=["par"], slot="c2")
        P.op("dve", lambda e: e.tensor_copy(out=G["ident_bf"][:], in_=G["cst"][:, 0:128]), reads=["cst"], writes=["ident_bf"])
        P.op("dve", lambda e: e.tensor_copy(out=G["ones_bf"][:], in_=G["cst"][:, 256:384]), reads=["cst"], writes=["XK"])
        P.op("pool", lambda e: e.memset(G["QT"][:], 0.0), writes=["QT"])
        P.op("pool", lambda e: e.memset(G["ones3"][:], 0.0), writes=["XK"])
        P.op("pool", lambda e: e.memset(G["ones3"][0:3, :], 1.0), writes=["XK"])
        P.op("pool", lambda e: e.memset(G["epsb"][:], EPS), writes=["epsb"])
        P.op("pool", lambda e: e.memset(G["oneb"][:], 1.0), writes=["oneb"])
        for li in range(n_layers_here):
            P.barrier()
            P.dma("sp", G["vec"][:], vec_d[li], writes=["vec"], slot="vecl")
            last = is_final and (li == n_layers_here - 1)
            if "fox" in phases:
                phase_fox(C, li, hfull, hmine)
            if "mla" in phases:
                phase_mla(C, li, hfull, hmine)
            if "swa" in phases:
                phase_swa(C, li, hfull, hmine)
            if "final" in phases:
                phase_final(C, li, hmine, out_d, last)
        P.barrier()
    C.n_inst = P.n_inst
    return nc


def build_fused():
    n_layers_here, is_final, phases, ybr_kind = 2, True, ("fox", "mla", "swa", "final"), "Internal"
    nc = bass.Bass("TRN2", target_bir_lowering=False)
    C = Ctx()
    C.nc = nc
    C.full = False
    C.ns = NS
    C.pfx = ""
    hfull = nc.dram_tensor("hfull", [NB, 128, D], F32, kind="ExternalInput").ap()
    h1 = nc.dram_tensor("h1_s", [NB, 128, D], F32, kind="Internal").ap()
    hm2 = nc.dram_tensor("hm2_s", [NS, 128, D], F32, kind="Internal").ap()
    C.w_in = nc.dram_tensor("w_in", [n_layers_here, D, NIN], F32, kind="ExternalInput").ap()
    C.w_uq = nc.dram_tensor("w_uq", [n_layers_here, 384, 768], F32, kind="ExternalInput").ap()
    C.w_ukv = nc.dram_tensor("w_ukv", [n_layers_here, 256, 1024], F32, kind="ExternalInput").ap()
    C.w_branch = nc.dram_tensor("w_branch", [n_layers_here, 3, 512, D], F32, kind="ExternalInput").ap()
    C.w_out = nc.dram_tensor("w_out", [n_layers_here, D, D], F32, kind="ExternalInput").ap()
    vec_d = nc.dram_tensor("vecs", [n_layers_here, 128, V_FG], F32, kind="ExternalInput").ap()
    C.fg_d = nc.dram_tensor("fg", [128, D], F32, kind="ExternalInput").ap()
    cst_d = nc.dram_tensor("cst", [128, 384], F32, kind="ExternalInput").ap()
    masks_d = nc.dram_tensor("masks", [128, NMASK, 128], BF16, kind="ExternalInput").ap()
    par_d = nc.dram_tensor("par", [128, 1], F32, kind="ExternalInput").ap()
    C.ropeK64_d = nc.dram_tensor("ropeK64", [128, NB, 64], F32, kind="ExternalInput").ap()
    C.ropeK32_d = nc.dram_tensor("ropeK32", [128, NB, 32], F32, kind="ExternalInput").ap()
    C.ropeQ64_d = nc.dram_tensor("ropeQ64", [128, NS, 64], F32, kind="ExternalInput").ap()
    C.ropeQ32_d = nc.dram_tensor("ropeQ32", [128, NS, 32], F32, kind="ExternalInput").ap()
    out_d = nc.dram_tensor("out", [NS, 128, D], F32, kind="ExternalOutput").ap()
    C.hnA = nc.dram_tensor("hnA_s", [8, 128, TK], BF16, kind="Internal").ap()
    C.hnM = nc.dram_tensor("hnM_s", [8, 128, TQ], BF16, kind="Internal").ap()
    C.ybr = nc.dram_tensor("ybr_in" if ybr_kind == "ExternalInput" else "ybr_s", [3, 4, 128, TK], BF16, kind=ybr_kind).ap()

    with ExitStack() as st:
        P = Prog(nc, st)
        C.P = P
        G = {}
        C.G = G
        sb = lambda name, shape, dt: st.enter_context(nc.sbuf_tensor(name, shape, dt))
        G["pf"] = [st.enter_context(nc.psum_tensor(f"pf{i}", [128, 512], F32)) for i in range(6)]
        G["tp"] = [st.enter_context(nc.psum_tensor(f"tp{i}", [128, 1024], BF16)) for i in range(2)]
        G["vec"] = sb("vec", [128, V_FG], F32)
        G["cst"] = sb("cstt", [128, 384], F32)
        G["masks"] = sb("maskt", [128, NMASK, 128], BF16)
        G["ident_bf"] = sb("identb", [128, 128], BF16)
        G["ones_bf"] = sb("onesb", [128, 128], BF16)
        G["par"] = sb("part", [128, 1], F32)
        G["epsb"] = sb("epsb", [128, 1], F32)
        G["oneb"] = sb("oneb", [128, 1], F32)
        G["xt"] = [sb(f"xt{i}", [128, D], F32) for i in range(2)]
        G["hnb"] = [sb(f"hnb{i}", [128, D], BF16) for i in range(2)]
        G["ss"] = sb("ss", [128, 4], F32)
        G["rs"] = sb("rs", [128, 4], F32)
        G["hnT"] = [sb(f"hnT{i}", [128, 8, 512], BF16) for i in range(2)]
        G["rtmp"] = sb("rtmp", [128, 512], F32)
        G["QT"] = sb("QT", [128, 8, 512], BF16)
        G["XQ"] = sb("XQ", [128, 8, 512], BF16)
        G["ones3"] = sb("ones3", [128, 128], BF16)
        G["Pt"] = [sb(f"Pt{i}", [128, 512], BF16) for i in range(3)]
        G["yg"] = sb("yg", [128, 4, 512], BF16)
        G["sz"] = sb("sz", [128, 512], BF16)
        G["rden"] = sb("rden", [128, 512], F32)
        G["bcs"] = sb("bcs", [64, 512], F32)
        G["rden2"] = [sb(f"rden2_{i}", [128, 512], F32) for i in range(2)]
        G["bcs2"] = [sb(f"bcs2_{i}", [64, 512], F32) for i in range(2)]
        C.srot = Rot([0, 1, 2])
        C.orot = Rot([3, 4])
        C.ptrot = Rot([0, 1, 2])

        P.dma("sp", G["cst"][:], cst_d, writes=["cst"], slot="c0")
        P.dma("sp", G["masks"][:], masks_d, writes=["masks"], slot="c1")
        P.dma("sp", G["par"][:], par_d, writes=["par"], slot="c2")
        P.op("dve", lambda e: e.tensor_copy(out=G["ident_bf"][:], in_=G["cst"][:, 0:128]), reads=["cst"], writes=["ident_bf"])
        P.op("dve", lambda e: e.tensor_copy(out=G["ones_bf"][:], in_=G["cst"][:, 256:384]), reads=["cst"], writes=["XK"])
        P.op("pool", lambda e: e.memset(G["QT"][:], 0.0), writes=["QT"])
        P.op("pool", lambda e: e.memset(G["ones3"][:], 0.0), writes=["XK"])
        P.op("pool", lambda e: e.memset(G["ones3"][0:3, :], 1.0), writes=["XK"])
        P.op("pool", lambda e: e.memset(G["epsb"][:], EPS), writes=["epsb"])
        P.op("pool", lambda e: e.memset(G["oneb"][:], 1.0), writes=["oneb"])
        G["parn"] = sb("parn", [128, 1], F32)
        P.op("dve", lambda e: e.tensor_scalar(out=G["parn"][:], in0=G["par"][:], scalar1=-1.0, scalar2=1.0, op0=ALU.mult, op1=ALU.add), reads=["par"], writes=["parn"])
        P.barrier()
        P.dma("sp", G["vec"][:], vec_d[0], writes=["vec"], slot="vecl")
        C.full, C.ns, C.pfx = True, NB - 1, "a_"
        phase_fox(C, 0, hfull, hfull)
        phase_mla(C, 0, hfull, hfull)
        phase_swa(C, 0, hfull, hfull)
        phase_final(C, 0, hfull, h1, False)
        P.barrier()
        P.op("pool", lambda e: e.memset(G["xt"][0][:], 0.0), writes=["xt0"])
        P.dma("sp", h1[NB - 1], G["xt"][0][:], reads=["xt0"], writes=["h1z"], slot="outw")
        P.barrier()
        with ExitStack() as ts:
            tb = [[ts.enter_context(nc.sbuf_tensor(f"sel{r}{t}", [128, D], F32)) for t in "ab"] for r in range(3)]

            def sel_ld(j):
                r = j % 3
                P.dma("sp", tb[r][0][:], h1[2 * j], writes=[f"sela{r}"], slot=f"sel{r}")
                P.dma("sp", tb[r][1][:], h1[2 * j + 1], writes=[f"selb{r}"], slot=f"sel{r}")

            def sel_cs(j):
                r = j % 3
                xa, xb_ = tb[r]
                ka, kb_ = f"sela{r}", f"selb{r}"
                P.op("dve", lambda e: e.tensor_scalar(out=xa[:], in0=xa[:], scalar1=G["parn"][:, 0:1], scalar2=None, op0=ALU.mult), reads=[ka, kb_, "parn"], writes=[ka])
                P.op("dve", lambda e: e.scalar_tensor_tensor(out=xa[:], in0=xb_[:], scalar=G["par"][:, 0:1], in1=xa[:], op0=ALU.mult, op1=ALU.add), reads=[ka, kb_, "par"], writes=[ka])
                P.dma("sp", hm2[j], xa[:], reads=[ka], writes=[f"hm2_{j}"], slot="outw")

            sel_ld(0)
            sel_ld(1)
            for j in range(NS):
                if j + 2 < NS:
                    sel_ld(j + 2)
                sel_cs(j)
            P.barrier()
        P.barrier()
        P.dma("sp", G["vec"][:], vec_d[1], writes=["vec"], slot="vecl")
        C.full, C.ns, C.pfx = False, NS, "b_"
        phase_fox(C, 1, h1, hm2)
        phase_mla(C, 1, h1, hm2)
        phase_swa(C, 1, h1, hm2)
        phase_final(C, 1, hm2, out_d, True)
        P.barrier()
    C.n_inst = P.n_inst
    return nc


def _rope_tab(pos, half):
    inv = (10000.0 ** (-np.arange(half, dtype=np.float32) / half)).astype(np.float32)
    ang = pos.astype(np.float32)[:, None] * inv[None, :]
    return np.concatenate([np.cos(ang), np.sin(ang)], axis=1).astype(np.float32)


def _const_inputs():
    k = np.arange(128)[:, None]
    q = np.arange(128)[None, :]
    tri = (k <= q)
    prev = (k > q)
    padk = (k >= PADN) & (q >= 0)
    padfix = tri & ((k >= PADN) | (q < PADN))
    allv = np.ones((128, 128), bool)
    none = np.zeros((128, 128), bool)
    per_par = []
    for c in range(2):
        m = np.zeros((128, NMASK, 128), np.float32)
        valid = {
            M_E: tri if c == 0 else allv,
            M_O: none if c == 0 else tri,
            M_E0: padfix if c == 0 else padk,
            M_P: padk,
            M_SP: prev if c == 0 else none,
            M_SE: tri if c == 0 else prev,
            M_SE0: padfix if c == 0 else (prev & padk),
            M_TRI: tri, M_E0F: padfix, M_PREV: prev, M_PREVP: prev & padk,
        }
        for i, v in valid.items():
            m[:, i, :] = np.where(v, 0.0, MASKV)
        per_par.append(m.astype(ml_dtypes.bfloat16))
    cst = np.concatenate([np.eye(128, dtype=np.float32), tri.astype(np.float32), np.ones((128, 128), np.float32)], axis=1)
    pos = np.arange(TK) - PADN
    r64 = _rope_tab(pos, 32).reshape(NB, 128, 64)
    r32 = _rope_tab(pos, 16).reshape(NB, 128, 32)
    return per_par, cst, r64, r32


def _blocks_for(c):
    return [2 * j + c for j in range(NS)]


def _run_layers(hfull_list, hmine_list, inputs, layers, is_final, phases=("fox", "mla", "swa", "final"), ybr_kind="Internal", ybr_in=None):
    per_par, cst, r64, r32 = _const_inputs()
    nl = len(layers)
    nc = build_program(nl, is_final, phases=phases, ybr_kind=ybr_kind)
    vecs = np.zeros((nl, 128, V_FG), np.float32)
    for i, l in enumerate(layers):
        row = np.concatenate([inputs["norm_g"][l], inputs["g_cq"][l], inputs["g_ckv"][l], inputs["b_f"][l], inputs["sinks"][l]]).astype(np.float32)
        vecs[i] = np.broadcast_to(row[None, :], (128, V_FG))
    fg = np.ascontiguousarray(np.broadcast_to(inputs["final_g"][None, :], (128, D))).astype(np.float32)
    in_maps = []
    for core in range(8):
        c = core % 2
        blks = _blocks_for(c)
        in_maps.append({
            "hfull": hfull_list[core // 2],
            "hmine": hmine_list[core],
            "w_in": np.ascontiguousarray(inputs["w_in"][layers]),
            "w_uq": np.ascontiguousarray(inputs["w_uq"][layers]),
            "w_ukv": np.ascontiguousarray(inputs["w_ukv"][layers]),
            "w_branch": np.ascontiguousarray(inputs["w_branch"][layers]),
            "w_out": np.ascontiguousarray(inputs["w_out"][layers]),
            "vecs": vecs,
            "fg": fg,
            "cst": cst,
            "masks": per_par[c],
            "par": np.full((128, 1), float(c), np.float32),
            "ropeK64": np.ascontiguousarray(r64.transpose(1, 0, 2)),
            "ropeK32": np.ascontiguousarray(r32.transpose(1, 0, 2)),
            "ropeQ64": np.ascontiguousarray(r64[blks].transpose(1, 0, 2)),
            "ropeQ32": np.ascontiguousarray(r32[blks].transpose(1, 0, 2)),
        })
    if ybr_in is not None:
        for core in range(8):
            in_maps[core]["ybr_in"] = ybr_in[core]
    res = run_bass_kernel_spmd(nc, in_maps, core_ids=list(range(8)))
    if ybr_kind == "ExternalOutput":
        return [r["ybr_s"] for r in res.results]
    return [r["out"] for r in res.results]


def _split(h):
    B = h.shape[0]
    hb = h.reshape(B, NB, 128, D)
    hfull = [np.ascontiguousarray(hb[b]) for b in range(B)]
    hmine = []
    for core in range(2 * B):
        hmine.append(np.ascontiguousarray(hb[core // 2][_blocks_for(core % 2)]))
    return hfull, hmine


def kernel_2launch(x, meta_tokens, norm_g, w_in, b_f, g_cq, g_ckv, w_uq, w_ukv, sinks, w_branch, w_out, final_g):
    inputs = dict(norm_g=np.asarray(norm_g, np.float32), w_in=np.asarray(w_in, np.float32), b_f=np.asarray(b_f, np.float32),
                  g_cq=np.asarray(g_cq, np.float32), g_ckv=np.asarray(g_ckv, np.float32), w_uq=np.asarray(w_uq, np.float32),
                  w_ukv=np.asarray(w_ukv, np.float32), sinks=np.asarray(sinks, np.float32), w_branch=np.asarray(w_branch, np.float32),
                  w_out=np.asarray(w_out, np.float32), final_g=np.asarray(final_g, np.float32))
    x = np.asarray(x, np.float32)
    B = x.shape[0]
    h = np.zeros((B, TK, D), np.float32)
    h[:, PADN:128] = np.asarray(meta_tokens, np.float32)[None]
    h[:, 128:128 + 4096] = x
    cur = h
    for l in range(2):
        hfull, hmine = _split(cur)
        outs = _run_layers(hfull, hmine, inputs, [l], is_final=(l == 1))
        nxt = np.zeros((B, NB, 128, D), np.float32)
        for core in range(8):
            nxt[core // 2, _blocks_for(core % 2)] = outs[core]
        nxt[:, NB - 1] = 0.0
        cur = nxt.reshape(B, TK, D)
    return np.ascontiguousarray(cur[:, 128:128 + 4096])


def kernel(x, meta_tokens, norm_g, w_in, b_f, g_cq, g_ckv, w_uq, w_ukv, sinks, w_branch, w_out, final_g):
    f = lambda a: np.ascontiguousarray(np.asarray(a, np.float32))
    x = f(x)
    B = x.shape[0]
    h = np.zeros((B, TK, D), np.float32)
    h[:, PADN:128] = f(meta_tokens)[None]
    h[:, 128:128 + 4096] = x
    hb = h.reshape(B, NB, 128, D)
    per_par, cst, r64, r32 = _const_inputs()
    vecs = np.zeros((2, 128, V_FG), np.float32)
    for l in range(2):
        row = np.concatenate([f(norm_g)[l], f(g_cq)[l], f(g_ckv)[l], f(b_f)[l], f(sinks)[l]])
        vecs[l] = np.broadcast_to(row[None, :], (128, V_FG))
    fg = np.ascontiguousarray(np.broadcast_to(f(final_g)[None, :], (128, D)))
    nc = build_fused()
    in_maps = []
    for core in range(8):
        c = core % 2
        blks = _blocks_for(c)
        in_maps.append({
            "hfull": np.ascontiguousarray(hb[core // 2]),
            "w_in": f(w_in), "w_uq": f(w_uq), "w_ukv": f(w_ukv), "w_branch": f(w_branch), "w_out": f(w_out),
            "vecs": vecs, "fg": fg, "cst": cst, "masks": per_par[c],
            "par": np.full((128, 1), float(c), np.float32),
            "ropeK64": np.ascontiguousarray(r64.transpose(1, 0, 2)),
            "ropeK32": np.ascontiguousarray(r32.transpose(1, 0, 2)),
            "ropeQ64": np.ascontiguousarray(r64[blks].transpose(1, 0, 2)),
            "ropeQ32": np.ascontiguousarray(r32[blks].transpose(1, 0, 2)),
        })
    res = run_bass_kernel_spmd(nc, in_maps, core_ids=list(range(8)))
    full = np.zeros((B, NB, 128, D), np.float32)
    for core in range(8):
        full[core // 2, _blocks_for(core % 2)] = res.results[core]["out"]
    return np.ascontiguousarray(full.reshape(B, TK, D)[:, 128:128 + 4096])
```
